# Optimizing a Trainium2 kernel written in Bass

```python
import jax, jax.numpy as jnp
from jax import lax
import numpy as np

D_MODEL = 1024
BATCH = 8
SEQ = 4096
DEPTH = 4

N_MIXERS = 2
N_SSD_LAYERS = (DEPTH + 1) // 2
N_SB_LAYERS = DEPTH // 2
NORM_EPS = 1e-6

SSD_EXPAND = 2
SSD_D_INNER = SSD_EXPAND * D_MODEL
SSD_HEAD_DIM = 64
SSD_HEADS = SSD_D_INNER // SSD_HEAD_DIM
SSD_GROUPS = 8
SSD_HEADS_PER_GROUP = SSD_HEADS // SSD_GROUPS
SSD_STATE = 128
SSD_CONV = 4
SSD_CHUNK = 128
SSD_CONV_DIM = SSD_D_INNER + 2 * SSD_GROUPS * SSD_STATE
SSD_IN_DIM = SSD_D_INNER + SSD_CONV_DIM + SSD_HEADS
SSD_DT_MIN = 1e-3
SSD_DT_MAX = 1e-1

SB_HEADS = 16
SB_HEAD_DIM = D_MODEL // SB_HEADS
SB_Q_BLOCK = 128

FFN_D_FF = 2816
FFN_CONV = 3

kernel_name = "hybrid_ssd_stickbreaking_convffn"


def rms_norm(x, g):
    xf = x.astype(jnp.float32)
    y = xf * lax.rsqrt(jnp.mean(xf * xf, axis=-1, keepdims=True) + NORM_EPS)
    return (y * g.astype(jnp.float32)).astype(x.dtype)


def causal_dwconv(u, w, b):
    width = w.shape[0]
    s = u.shape[1]
    up = jnp.pad(u, ((0, 0), (width - 1, 0), (0, 0)))
    return b + sum(w[k] * up[:, k:k + s] for k in range(width))


def ssd_chunked_scan(xdt, a, bm, cm):
    bsz, s = xdt.shape[:2]
    L, G, HG, P, N = SSD_CHUNK, SSD_GROUPS, SSD_HEADS_PER_GROUP, SSD_HEAD_DIM, SSD_STATE
    nc = s // L

    def to_chunks(t):
        t = t.reshape((bsz, nc, L) + t.shape[2:])
        return jnp.moveaxis(t, 1, 0)

    xc = to_chunks(xdt.reshape(bsz, s, G, HG, P))
    ac = to_chunks(a.reshape(bsz, s, G, HG))
    bc = to_chunks(bm)
    cc = to_chunks(cm)
    causal = jnp.tril(jnp.ones((L, L), dtype=bool))[None, :, :, None, None]

    def step(state, inp):
        x_c, a_c, b_c, c_c = inp
        acum = jnp.cumsum(a_c, axis=1)
        seg = acum[:, :, None] - acum[:, None, :]
        decay = jnp.exp(jnp.where(causal, seg, -jnp.inf))
        cb = jnp.einsum('btgn,bsgn->btsg', c_c, b_c)
        scores = cb[..., None] * decay
        y_diag = jnp.einsum('btsgh,bsghp->btghp', scores, x_c)
        y_off = jnp.einsum('btgn,bghpn->btghp', c_c, state) * jnp.exp(acum)[..., None]
        last = acum[:, -1]
        w_s = jnp.exp(last[:, None] - acum)
        new_state = (state * jnp.exp(last)[..., None, None]
                     + jnp.einsum('bsgn,bsghp->bghpn', b_c, x_c * w_s[..., None]))
        return new_state, (y_diag + y_off).astype(jnp.float32)

    state0 = jnp.zeros((bsz, G, HG, P, N), jnp.float32)
    _, ys = lax.scan(step, state0, (xc, ac, bc, cc))
    return jnp.moveaxis(ys, 0, 1).reshape(bsz, s, SSD_HEADS, P)


def ssd_mixer(h, w_in, conv_w, conv_b, dt_bias, a_log, d_skip, norm_g, w_out):
    bsz, s, _ = h.shape
    proj = h @ w_in
    z, xbc, dt = jnp.split(proj, [SSD_D_INNER, SSD_D_INNER + SSD_CONV_DIM], axis=-1)
    xbc = jax.nn.silu(causal_dwconv(xbc, conv_w, conv_b))
    xs, bm, cm = jnp.split(xbc, [SSD_D_INNER, SSD_D_INNER + SSD_GROUPS * SSD_STATE], axis=-1)
    xs = xs.reshape(bsz, s, SSD_HEADS, SSD_HEAD_DIM)
    bm = bm.reshape(bsz, s, SSD_GROUPS, SSD_STATE)
    cm = cm.reshape(bsz, s, SSD_GROUPS, SSD_STATE)
    dt = jax.nn.softplus(dt.astype(jnp.float32) + dt_bias.astype(jnp.float32))
    a = -jnp.exp(a_log.astype(jnp.float32)) * dt
    y = ssd_chunked_scan(xs * dt[..., None], a, bm, cm)
    y = y + d_skip[:, None] * xs
    y = y.reshape(bsz, s, SSD_D_INNER).astype(h.dtype)
    y = rms_norm(y * jax.nn.silu(z), norm_g)
    return y @ w_out


def stick_breaking_mixer(h, w_qkv, w_out):
    bsz, s, _ = h.shape
    qkv = (h @ w_qkv).reshape(bsz, s, 3, SB_HEADS, SB_HEAD_DIM)
    q = jnp.moveaxis(qkv[:, :, 0], 1, 2)
    k = jnp.moveaxis(qkv[:, :, 1], 1, 2)
    v = jnp.moveaxis(qkv[:, :, 2], 1, 2)
    scale = SB_HEAD_DIM ** -0.5
    outs = []
    for blk in range(s // SB_Q_BLOCK):
        q0 = blk * SB_Q_BLOCK
        kv_end = q0 + SB_Q_BLOCK
        logits = jnp.einsum('bhqd,bhkd->bhqk', q[:, :, q0:kv_end], k[:, :, :kv_end]).astype(jnp.float32) * scale
        qpos = q0 + jnp.arange(SB_Q_BLOCK)[:, None]
        kpos = jnp.arange(kv_end)[None, :]
        strict = kpos < qpos
        log_beta = jax.nn.log_sigmoid(logits)
        log_fail = jnp.where(strict, jax.nn.log_sigmoid(-logits), 0.0)
        suffix = lax.cumsum(log_fail, axis=3, reverse=True) - log_fail
        weights = jnp.where(strict, jnp.exp(log_beta + suffix), 0.0)
        outs.append(jnp.einsum('bhqk,bhkd->bhqd', weights.astype(v.dtype), v[:, :, :kv_end]))
    o = jnp.concatenate(outs, axis=2)
    o = jnp.moveaxis(o, 1, 2).reshape(bsz, s, D_MODEL)
    return o @ w_out


def conv_ffn(h, w_in, conv_w, conv_b, w_out):
    u = causal_dwconv(h @ w_in, conv_w, conv_b)
    gate, up = jnp.split(u, 2, axis=-1)
    return (jax.nn.silu(gate) * up) @ w_out


def setup_inputs(seed: int = 0) -> dict:
    key = jax.random.key(seed)
    ks = jax.random.split(key, 24)
    f32 = jnp.float32
    out_scale = (2.0 * DEPTH) ** -0.5

    def nrm(k, shape, scale):
        return jax.random.normal(k, shape, f32) * scale

    x = jax.random.normal(ks[0], (BATCH, SEQ, D_MODEL), f32)
    mix_norm = 1.0 + nrm(ks[1], (DEPTH, D_MODEL), 0.02)
    ffn_norm = 1.0 + nrm(ks[2], (DEPTH, D_MODEL), 0.02)
    final_norm = 1.0 + nrm(ks[3], (D_MODEL,), 0.02)

    ssd_w_in = nrm(ks[4], (N_SSD_LAYERS, D_MODEL, SSD_IN_DIM), D_MODEL ** -0.5)
    ssd_conv_w = nrm(ks[5], (N_SSD_LAYERS, SSD_CONV, SSD_CONV_DIM), SSD_CONV ** -0.5)
    ssd_conv_b = nrm(ks[6], (N_SSD_LAYERS, SSD_CONV_DIM), 0.02)
    u = jax.random.uniform(ks[7], (N_SSD_LAYERS, SSD_HEADS), f32)
    dt0 = jnp.exp(u * (np.log(SSD_DT_MAX) - np.log(SSD_DT_MIN)) + np.log(SSD_DT_MIN))
    ssd_dt_bias = dt0 + jnp.log(-jnp.expm1(-dt0))
    ssd_a_log = jnp.log(jax.random.uniform(ks[8], (N_SSD_LAYERS, SSD_HEADS), f32, 1.0, 16.0))
    ssd_d = 1.0 + nrm(ks[9], (N_SSD_LAYERS, SSD_HEADS), 0.1)
    ssd_norm = 1.0 + nrm(ks[10], (N_SSD_LAYERS, SSD_D_INNER), 0.02)
    ssd_w_out = nrm(ks[11], (N_SSD_LAYERS, SSD_D_INNER, D_MODEL), SSD_D_INNER ** -0.5 * out_scale)

    sb_w_qkv = nrm(ks[12], (N_SB_LAYERS, D_MODEL, 3 * D_MODEL), D_MODEL ** -0.5)
    sb_w_out = nrm(ks[13], (N_SB_LAYERS, D_MODEL, D_MODEL), D_MODEL ** -0.5 * out_scale)

    ffn_w_in = nrm(ks[14], (DEPTH, D_MODEL, 2 * FFN_D_FF), D_MODEL ** -0.5)
    ffn_conv_w = nrm(ks[15], (DEPTH, FFN_CONV, 2 * FFN_D_FF), FFN_CONV ** -0.5)
    ffn_conv_b = nrm(ks[16], (DEPTH, 2 * FFN_D_FF), 0.02)
    ffn_w_out = nrm(ks[17], (DEPTH, FFN_D_FF, D_MODEL), FFN_D_FF ** -0.5 * out_scale)

    return {"x": x, "mix_norm": mix_norm, "ffn_norm": ffn_norm, "final_norm": final_norm,
            "ssd_w_in": ssd_w_in, "ssd_conv_w": ssd_conv_w, "ssd_conv_b": ssd_conv_b,
            "ssd_dt_bias": ssd_dt_bias, "ssd_a_log": ssd_a_log, "ssd_d": ssd_d,
            "ssd_norm": ssd_norm, "ssd_w_out": ssd_w_out,
            "sb_w_qkv": sb_w_qkv, "sb_w_out": sb_w_out,
            "ffn_w_in": ffn_w_in, "ffn_conv_w": ffn_conv_w, "ffn_conv_b": ffn_conv_b,
            "ffn_w_out": ffn_w_out}


def reference(x, mix_norm, ffn_norm, final_norm,
              ssd_w_in, ssd_conv_w, ssd_conv_b, ssd_dt_bias, ssd_a_log, ssd_d, ssd_norm, ssd_w_out,
              sb_w_qkv, sb_w_out,
              ffn_w_in, ffn_conv_w, ffn_conv_b, ffn_w_out):
    for i in range(DEPTH):
        h = rms_norm(x, mix_norm[i])
        j = i // N_MIXERS
        if i % N_MIXERS == 0:
            x = x + ssd_mixer(h, ssd_w_in[j], ssd_conv_w[j], ssd_conv_b[j], ssd_dt_bias[j],
                              ssd_a_log[j], ssd_d[j], ssd_norm[j], ssd_w_out[j])
        else:
            x = x + stick_breaking_mixer(h, sb_w_qkv[j], sb_w_out[j])
        x = x + conv_ffn(rms_norm(x, ffn_norm[i]), ffn_w_in[i], ffn_conv_w[i], ffn_conv_b[i], ffn_w_out[i])
    return rms_norm(x, final_norm)
```

```python
from contextlib import ExitStack
import numpy as np
import concourse.bass as bass
import concourse.mybir as mybir
from concourse.bass_utils import run_bass_kernel_spmd

F32 = mybir.dt.float32
BF16 = mybir.dt.bfloat16
AF = mybir.ActivationFunctionType
ALU = mybir.AluOpType

D = 1024
DC = 8
NORM_EPS = 1e-6
DFF = 2816
FC = DFF // 128
TT = 512
WSLOT = 4096
NWS = 3

SSD_DI = 2048
SSD_H = 32
SSD_P = 64
SSD_G = 8
SSD_N = 128
SSD_CONVD = 4096
SSD_IN = 6176
SB_H = 16
SB_DH = 64
NEG = -30000.0


class Buf:
    __slots__ = ("w", "r", "name")

    def __init__(self, name=""):
        self.w = None
        self.r = {}
        self.name = name


class Prog:
    ENGS = ("pe", "act", "dve", "pool", "sp")

    def __init__(self, nc, stack):
        self.nc = nc
        self.stack = stack
        self.streams = {e: [] for e in self.ENGS}
        self.sem = {e: stack.enter_context(nc.semaphore("s_" + e)) for e in self.ENGS}
        self.cnt = {e: 0 for e in self.ENGS}
        self.waited = {e: {} for e in self.ENGS}
        self.dmasems = {}
        self.dmacnt = {}
        self.same_engine_sync = True

    def _waits(self, eng, reads, writes):
        deps = {}

        def add(ev):
            if ev is None:
                return
            k, v = ev
            if deps.get(k, 0) < v:
                deps[k] = v

        for b in reads:
            add(b.w)
        for b in writes:
            add(b.w)
            for k, v in b.r.items():
                add((k, v))
        out = []
        for k, v in deps.items():
            if k == eng and (eng == "pe" or not self.same_engine_sync):
                continue
            if self.waited[eng].get(k, 0) >= v:
                continue
            self.waited[eng][k] = v
            out.append((k, v))
        return out

    def _commit(self, ev, reads, writes):
        k, v = ev
        for b in writes:
            b.w = ev
            b.r = {}
        for b in reads:
            if b.r.get(k, 0) < v:
                b.r[k] = v

    def op(self, eng, fn, reads=(), writes=()):
        waits = self._waits(eng, reads, writes)
        self.cnt[eng] += 1
        ev = (eng, self.cnt[eng])
        self.streams[eng].append((waits, fn, eng, 1))
        self._commit(ev, reads, writes)

    def dma(self, eng, fn, semname, reads=(), writes=()):
        if semname not in self.dmasems:
            self.dmasems[semname] = self.stack.enter_context(self.nc.semaphore("d_" + semname))
            self.dmacnt[semname] = 0
        waits = self._waits(eng, reads, writes)
        self.dmacnt[semname] += 16
        key = "D:" + semname
        ev = (key, self.dmacnt[semname])
        self.streams[eng].append((waits, fn, key, 16))
        self._commit(ev, reads, writes)

    def semh(self, key):
        if key.startswith("D:"):
            return self.dmasems[key[2:]]
        return self.sem[key]

    def barrier(self):
        evs = [(e, self.cnt[e]) for e in self.ENGS if self.cnt[e] > 0]
        evs += [("D:" + s, c) for s, c in self.dmacnt.items() if c > 0]
        for e in self.ENGS:
            waits = []
            for k, v in evs:
                if k == e:
                    continue
                if self.waited[e].get(k, 0) >= v:
                    continue
                self.waited[e][k] = v
                waits.append((k, v))
            if waits:
                self.streams[e].append((waits, None, None, 0))

    def emit(self):
        nc = self.nc
        engobj = {"pe": "tensor", "act": "scalar", "dve": "vector", "pool": "gpsimd", "sp": "sync"}
        with nc.Block() as block:
            for e in self.ENGS:
                stream = self.streams[e]

                def body(eng, stream=stream):
                    for waits, fn, key, inc in stream:
                        for k, v in waits:
                            eng.wait_ge(self.semh(k), v)
                        if fn is not None:
                            fn(eng).then_inc(self.semh(key), inc)

                getattr(block, engobj[e])(body)


class Ring:
    def __init__(self, items):
        self.items = items
        self.i = 0

    def next(self):
        it = self.items[self.i % len(self.items)]
        self.i += 1
        return it


def pack_small(inp):
    cols = {}
    arrs = []
    off = 0

    def put(name, a):
        nonlocal off
        a = np.ascontiguousarray(a, dtype=np.float32)
        assert a.shape[0] == 128
        cols[name] = (off, a.shape[1])
        arrs.append(a)
        off += a.shape[1]

    def pm(v, nchunk):
        v = np.asarray(v, np.float32)
        lead = v.shape[:-1]
        v = v.reshape(lead + (nchunk, 128))
        v = np.moveaxis(v, -1, 0)
        return v.reshape(128, -1)

    def bc(v):
        v = np.asarray(v, np.float32).reshape(1, -1)
        return np.broadcast_to(v, (128, v.shape[1]))

    put("mixg", pm(inp["mix_norm"], 8))
    put("ffng", pm(inp["ffn_norm"], 8))
    put("fing", pm(inp["final_norm"], 8))
    put("fcw", pm(inp["ffn_conv_w"], 44))
    put("fcb", pm(inp["ffn_conv_b"], 44))
    put("scw", pm(inp["ssd_conv_w"], 32))
    put("scb", pm(inp["ssd_conv_b"], 32))
    put("sng", pm(inp["ssd_norm"], 16))
    put("dtb", bc(inp["ssd_dt_bias"]))
    put("alog", bc(inp["ssd_a_log"]))
    put("dsk", bc(inp["ssd_d"]))
    return np.concatenate(arrs, axis=1), cols


def make_consts():
    k = np.arange(128)
    c = {}
    ident = np.eye(128, dtype=np.float32)
    negU = -(k[:, None] > k[None, :]).astype(np.float32)
    negL = -(k[:, None] <= k[None, :]).astype(np.float32)
    q = np.arange(512)
    am = np.stack([np.where(r * 128 + k[:, None] < q[None, :], 0.0, NEG) for r in range(4)], axis=1)
    cb16 = np.concatenate([ident, np.ones((128, 128), np.float32), negU, negL, am.reshape(128, -1)], axis=1)
    c["identb"] = (0, 128)
    c["onesb"] = (128, 128)
    c["negU"] = (256, 128)
    c["negL"] = (384, 128)
    c["amask"] = (512, 2048)
    triR = (k[:, None] <= k[None, :]).astype(np.float32)
    ustr = (k[:, None] > k[None, :]).astype(np.float32)
    cmask = (k[:, None] <= k[None, :]).astype(np.float32)
    c32 = np.concatenate([triR, ustr, cmask, np.ones((128, 128), np.float32), ident], axis=1)
    f = {"triR": (0, 128), "ustr": (128, 128), "cmask": (256, 128), "ones32": (384, 128), "ident32": (512, 128)}
    import ml_dtypes
    return cb16.astype(ml_dtypes.bfloat16), c, c32.astype(np.float32), f


def build_program(S, plan, small_cols, n_small, cst16_cols, n_c16, cst32_cols, n_c32,
                  do_final=True, x_from_input=True):
    NT = S // TT
    nc = bass.Bass("TRN2", target_bir_lowering=False)
    xin = nc.dram_tensor("xT", [D, S], F32, kind="ExternalInput").ap()
    outT = nc.dram_tensor("outT", [D, S], F32, kind="ExternalOutput").ap()
    small_d = nc.dram_tensor("small", [128, n_small], F32, kind="ExternalInput").ap()
    c16_d = nc.dram_tensor("c16", [128, n_c16], BF16, kind="ExternalInput").ap()
    c32_d = nc.dram_tensor("c32", [128, n_c32], F32, kind="ExternalInput").ap()
    xres = nc.dram_tensor("xres", [D, S], F32, kind="Internal").ap()
    xin_v = xin.rearrange("(dc p) s -> p dc s", p=128)
    xres_v = xres.rearrange("(dc p) s -> p dc s", p=128)
    out_v = outT.rearrange("(dc p) s -> p dc s", p=128)

    kinds = set(p[0] for p in plan)
    wd = {}
    if "ffn" in kinds:
        wd["ffn_w_in"] = nc.dram_tensor("ffn_w_in", [4, D, 2 * DFF], F32, kind="ExternalInput").ap()
        wd["ffn_w_out"] = nc.dram_tensor("ffn_w_out", [4, DFF, D], F32, kind="ExternalInput").ap()
    if "ssd" in kinds:
        wd["ssd_w_in"] = nc.dram_tensor("ssd_w_in", [2, D, SSD_IN], F32, kind="ExternalInput").ap()
        wd["ssd_w_out"] = nc.dram_tensor("ssd_w_out", [2, SSD_DI, D], F32, kind="ExternalInput").ap()
    if "sb" in kinds:
        wd["sb_w_qkv"] = nc.dram_tensor("sb_w_qkv", [2, D, 3 * D], F32, kind="ExternalInput").ap()
        wd["sb_w_out"] = nc.dram_tensor("sb_w_out", [2, D, D], F32, kind="ExternalInput").ap()

    st = ExitStack()
    with st:
        P = Prog(nc, st)

        def sb(name, shape, dt):
            return st.enter_context(nc.sbuf_tensor(name, shape, dt))

        small = sb("small_sb", [128, n_small], F32)
        c16 = sb("c16_sb", [128, n_c16], BF16)
        c32 = sb("c32_sb", [128, n_c32], F32)
        wslots = [sb(f"wslot{i}", [128, WSLOT], BF16) for i in range(NWS)]
        wbuf = [Buf(f"wslot{i}") for i in range(NWS)]
        xts = [sb(f"xt{i}", [128, DC, TT], F32) for i in range(2)]
        xtb = [Buf(f"xt{i}") for i in range(2)]
        sqb = [sb(f"sq{i}", [128, TT], BF16) for i in range(2)]
        sqbuf = [Buf() for _ in range(2)]
        rstd = sb("rstd", [128, TT], F32)
        rstdb = Buf("rstd")
        lnv = sb("lnv", [128, TT], F32)
        lnvb = Buf("lnv")
        psums = [st.enter_context(nc.psum_tensor(f"ps{i}", [128, 512], F32)) for i in range(8)]
        psb = [Buf(f"ps{i}") for i in range(8)]
        cbuf = Buf("consts")
        xres_b = [Buf(f"xres{i}") for i in range(NT)]

        def sm(name, idx=0, n=1):
            o, _ = small_cols[name]
            return small[:, o + idx:o + idx + n]

        def k16(name):
            o, n = cst16_cols[name]
            return c16[:, o:o + n]

        def k32(name):
            o, n = cst32_cols[name]
            return c32[:, o:o + n]

        P.dma("sp", lambda e: e.dma_start(out=small[:], in_=small_d), "cst", writes=[cbuf])
        P.dma("sp", lambda e: e.dma_start(out=c16[:], in_=c16_d), "cst", writes=[cbuf])
        P.dma("sp", lambda e: e.dma_start(out=c32[:], in_=c32_d), "cst", writes=[cbuf])

        wgroups = {}

        def conv_group(key, pieces, n, parts=128):
            scr = nc.dram_tensor("scr_" + "_".join(str(k) for k in key), [parts, n], BF16, kind="Internal").ap()
            b = Buf("scr")
            i = conv_group.i % NWS
            conv_group.i += 1
            slot = wslots[i]
            for src, dstf in pieces:
                P.dma("pool", lambda e, src=src, dstf=dstf, slot=slot: e.dma_start(out=dstf(slot), in_=src),
                      f"cv{i}", writes=[wbuf[i]])
            P.dma("sp", lambda e, slot=slot, scr=scr, n=n: e.dma_start(out=scr, in_=slot[0:parts, 0:n]),
                  f"cs{i}", reads=[wbuf[i]], writes=[b])
            wgroups[key] = (scr, b, n, parts)

        conv_group.i = 0

        def v3(slot, a, b, lo=None, hi=None):
            v = slot[:, 0:a * b].rearrange("p (a b) -> p a b", a=a)
            if lo is not None:
                v = v[:, :, lo:hi]
            return v

        for item in plan:
            if item[0] == "ffn":
                l = item[1]
                Wv = wd["ffn_w_in"][l].rearrange("(dc p) n -> p dc n", p=128)
                for g in range(11):
                    conv_group(("fi", l, g), [
                        (Wv[:, :, 256 * g:256 * g + 256], lambda s: v3(s, 8, 512, 0, 256)),
                        (Wv[:, :, DFF + 256 * g:DFF + 256 * g + 256], lambda s: v3(s, 8, 512, 256, 512)),
                    ], 4096)
                Wo = wd["ffn_w_out"][l].rearrange("(fc p) n -> p fc n", p=128)
                for j in range(8):
                    conv_group(("fo", l, j), [(Wo[:, :, 128 * j:128 * j + 128], lambda s: v3(s, FC, 128))], FC * 128)
            if item[0] == "ssd":
                j = item[1]
                Wv = wd["ssd_w_in"][j].rearrange("(dc p) n -> p dc n", p=128)
                for zg in range(4):
                    conv_group(("sz", j, zg), [(Wv[:, :, 512 * zg:512 * zg + 512], lambda s: v3(s, 8, 512))], 4096)
                for xg in range(8):
                    conv_group(("sx", j, xg), [(Wv[:, :, 2048 + 512 * xg:2048 + 512 * xg + 512], lambda s: v3(s, 8, 512))], 4096)
                conv_group(("sdt", j), [(Wv[:, :, 6144:6176], lambda s: v3(s, 8, 32))], 256)
                Wo = wd["ssd_w_out"][j].rearrange("(cc p) n -> p cc n", p=128)
                for k in range(4):
                    conv_group(("sout", j, k), [(Wo[:, :, 256 * k:256 * k + 256], lambda s: v3(s, 16, 256))], 4096)
            if item[0] == "sb":
                j = item[1]
                Wv = wd["sb_w_qkv"][j].rearrange("(dc p) n -> p dc n", p=128)
                for hp in range(8):
                    conv_group(("sq", j, hp), [
                        (Wv[:, :, 128 * hp:128 * hp + 128], lambda s: v3(s, 8, 384, 0, 128)),
                        (Wv[:, :, D + 128 * hp:D + 128 * hp + 128], lambda s: v3(s, 8, 384, 128, 256)),
                        (Wv[:, :, 2 * D + 128 * hp:2 * D + 128 * hp + 128], lambda s: v3(s, 8, 384, 256, 384)),
                    ], 3072)
                Wo = wd["sb_w_out"][j].rearrange("(h p) n -> p h n", p=64)
                for jn in range(8):
                    conv_group(("so", j, jn), [
                        (Wo[:, :, 128 * jn:128 * jn + 128],
                         lambda s: s[0:64, 0:2048].rearrange("p (h n) -> p h n", h=16))], 2048, parts=64)

        wsched = []
        wstate = {"issued": 0, "used": 0}

        def w_issue_upto(n):
            while wstate["issued"] < min(n, len(wsched)):
                i = wstate["issued"]
                key = wsched[i]
                scr, b, ncol, parts = wgroups[key]
                s = i % NWS
                P.dma("sp", lambda e, s=s, scr=scr, ncol=ncol, parts=parts: e.dma_start(out=wslots[s][0:parts, 0:ncol], in_=scr),
                      f"wl{s}", reads=[b], writes=[wbuf[s]])
                wstate["issued"] += 1

        def next_w(key):
            i = wstate["used"]
            assert wsched[i] == key, (wsched[i], key)
            w_issue_upto(i + NWS)
            wstate["used"] += 1
            s = i % NWS
            return wslots[s], wbuf[s]

        for item in plan:
            if item[0] == "ffn":
                l = item[1]
                for it in range(NT):
                    wsched.extend(("fi", l, g) for g in range(11))
                    wsched.extend(("fo", l, j) for j in range(8))
            if item[0] == "ssd":
                j = item[1]
                for it in range(NT):
                    wsched.extend(("sx", j, xg) for xg in range(8))
                    wsched.extend(("sz", j, zg) for zg in range(4))
                    wsched.append(("sdt", j))
                    for blk in range(4):
                        wsched.extend(("sout", j, k) for k in range(4))
            if item[0] == "sb":
                j = item[1]
                wsched.extend(("sq", j, hp) for hp in range(8))
                for it in range(NT):
                    wsched.extend(("so", j, jn) for jn in range(8))

        xstate = {"n": 0}

        def load_x(src_v, it, srcbuf, slot=None):
            i = xstate["n"] % 2 if slot is None else slot
            xstate["n"] += 1
            P.dma("sp", lambda e, i=i, it=it: e.dma_start(out=xts[i][:], in_=src_v[:, :, it * TT:(it + 1) * TT]),
                  f"xl{i}", reads=[srcbuf] if srcbuf is not None else [], writes=[xtb[i]])
            return i

        def store_x(dst_v, it, i, dstbuf):
            P.dma("pool", lambda e, i=i, it=it: e.dma_start(out=dst_v[:, :, it * TT:(it + 1) * TT], in_=xts[i][:]),
                  f"xs{i}", reads=[xtb[i]], writes=[dstbuf] if dstbuf is not None else [])

        ps_ring = Ring([0, 1, 2, 3])
        ps_ring2 = Ring([4, 5])
        PS_NORM = 6

        def rms_stats(xi):
            xt = xts[xi]
            for dc in range(DC):
                q = dc % 2
                P.op("act", lambda e, dc=dc, q=q: e.activation(out=sqb[q][:], in_=xt[:, dc, :], func=AF.Square),
                     reads=[xtb[xi]], writes=[sqbuf[q]])
                P.op("pe", lambda e, dc=dc, q=q: e.matmul(psums[PS_NORM][:], k16("onesb"), sqb[q][:],
                                                          start=(dc == 0), stop=(dc == DC - 1)),
                     reads=[sqbuf[q], cbuf], writes=[psb[PS_NORM]])
            P.op("act", lambda e: e.activation(out=lnv[:], in_=psums[PS_NORM][:], func=AF.Ln,
                                               scale=1.0 / D, bias=sm("eps")),
                 reads=[psb[PS_NORM], cbuf], writes=[lnvb])
            P.op("act", lambda e: e.activation(out=rstd[:], in_=lnv[:], func=AF.Exp, scale=-0.5),
                 reads=[lnvb], writes=[rstdb])

        def rms_apply(xi, gname, gidx, out_t, out_b):
            xt = xts[xi]
            for dc in range(DC):
                P.op("dve", lambda e, dc=dc: e.scalar_tensor_tensor(
                    out=out_t[:, dc, :], in0=xt[:, dc, :], scalar=sm(gname, gidx + dc), in1=rstd[:],
                    op0=ALU.mult, op1=ALU.mult),
                    reads=[xtb[xi], rstdb, cbuf], writes=[out_b])

        def ffn_phase(l, src_v, srcbufs, dst_v, dstbufs):
            ph = ExitStack()
            with ph:
                hT = ph.enter_context(nc.sbuf_tensor(f"hT{l}", [128, DC, TT], BF16))
                hTb = Buf("hT")
                gT = ph.enter_context(nc.sbuf_tensor(f"gT{l}", [128, FC, TT], BF16))
                gTb = [Buf() for _ in range(FC)]
                tails = ph.enter_context(nc.sbuf_tensor(f"tails{l}", [128, 2 * FC, 2], F32))
                tailb = Buf("tails")
                ubs = [ph.enter_context(nc.sbuf_tensor(f"ub{l}_{i}", [128, TT + 2], F32)) for i in range(2)]
                ubb = [Buf() for _ in range(2)]
                accs = [ph.enter_context(nc.sbuf_tensor(f"acc{l}_{i}", [128, TT], F32)) for i in range(4)]
                accb = [Buf() for _ in range(4)]
                sgs = [ph.enter_context(nc.sbuf_tensor(f"sg{l}_{i}", [128, TT], F32)) for i in range(2)]
                sgb = [Buf() for _ in range(2)]
                ubr = Ring([0, 1])
                sgr = Ring([0, 1])
                P.op("pool", lambda e: e.memset(tails[:], 0.0), writes=[tailb])
                cur = load_x(src_v, 0, srcbufs[0])
                for it in range(NT):
                    xi = cur
                    if it + 1 < NT:
                        cur = load_x(src_v, it + 1, srcbufs[it + 1])
                    rms_stats(xi)
                    rms_apply(xi, "ffng", l * 8, hT, hTb)
                    for g in range(11):
                        wt, wtb = next_w(("fi", l, g))
                        wv = v3(wt, 8, 512)
                        for c in range(4):
                            ch = 2 * g + c if c < 2 else FC + 2 * g + (c - 2)
                            pi = ps_ring.next()
                            for dc in range(DC):
                                P.op("pe", lambda e, dc=dc, c=c, pi=pi, wv=wv: e.matmul(
                                    psums[pi][:], wv[:, dc, c * 128:(c + 1) * 128], hT[:, dc, :],
                                    start=(dc == 0), stop=(dc == DC - 1)),
                                    reads=[wtb, hTb], writes=[psb[pi]])
                            ui = ubr.next()
                            ub = ubs[ui]
                            acc = accs[c]
                            P.op("dve", lambda e, ub=ub, ch=ch: e.tensor_copy(out=ub[:, 0:2], in_=tails[:, ch, :]),
                                 reads=[tailb], writes=[ubb[ui]])
                            P.op("act", lambda e, ub=ub, pi=pi: e.activation(out=ub[:, 2:TT + 2], in_=psums[pi][:],
                                                                             func=AF.Copy),
                                 reads=[psb[pi]], writes=[ubb[ui]])
                            P.op("dve", lambda e, ub=ub, ch=ch: e.tensor_copy(out=tails[:, ch, :], in_=ub[:, TT:TT + 2]),
                                 reads=[ubb[ui]], writes=[tailb])
                            w0 = sm("fcw", (l * 3 + 0) * 44 + ch)
                            w1 = sm("fcw", (l * 3 + 1) * 44 + ch)
                            w2 = sm("fcw", (l * 3 + 2) * 44 + ch)
                            bb = sm("fcb", l * 44 + ch)
                            P.op("pool", lambda e, ub=ub, acc=acc, w2=w2, bb=bb: e.tensor_scalar(
                                out=acc[:], in0=ub[:, 2:TT + 2], scalar1=w2, scalar2=bb, op0=ALU.mult, op1=ALU.add),
                                reads=[ubb[ui], cbuf], writes=[accb[c]])
                            P.op("dve", lambda e, ub=ub, acc=acc, w1=w1: e.scalar_tensor_tensor(
                                out=acc[:], in0=ub[:, 1:TT + 1], scalar=w1, in1=acc[:], op0=ALU.mult, op1=ALU.add),
                                reads=[ubb[ui], accb[c], cbuf], writes=[accb[c]])
                            P.op("dve", lambda e, ub=ub, acc=acc, w0=w0: e.scalar_tensor_tensor(
                                out=acc[:], in0=ub[:, 0:TT], scalar=w0, in1=acc[:], op0=ALU.mult, op1=ALU.add),
                                reads=[ubb[ui], accb[c], cbuf], writes=[accb[c]])
                        for c in range(2):
                            si = sgr.next()
                            fi = 2 * g + c
                            P.op("act", lambda e, si=si, c=c: e.activation(out=sgs[si][:], in_=accs[c][:], func=AF.Silu),
                                 reads=[accb[c]], writes=[sgb[si]])
                            P.op("pool", lambda e, si=si, c=c, fi=fi: e.tensor_tensor(
                                out=gT[:, fi, :], in0=sgs[si][:], in1=accs[2 + c][:], op=ALU.mult),
                                reads=[sgb[si], accb[2 + c]], writes=[gTb[fi]])
                    xt = xts[xi]
                    for j in range(8):
                        wt, wtb = next_w(("fo", l, j))
                        wv = v3(wt, FC, 128)
                        pi = ps_ring2.next()
                        for fc in range(FC):
                            P.op("pe", lambda e, fc=fc, pi=pi, wv=wv: e.matmul(
                                psums[pi][:], wv[:, fc, :], gT[:, fc, :], start=(fc == 0), stop=(fc == FC - 1)),
                                reads=[wtb, gTb[fc]], writes=[psb[pi]])
                        P.op("dve", lambda e, j=j, pi=pi, xt=xt: e.tensor_tensor(
                            out=xt[:, j, :], in0=psums[pi][:], in1=xt[:, j, :], op=ALU.add),
                            reads=[psb[pi], xtb[xi]], writes=[xtb[xi]])
                    store_x(dst_v, it, xi, dstbufs[it])
                P.barrier()

        def sb_phase(j, l, src_v, srcbufs, dst_v, dstbufs):
            NB = S // 128
            NG = S // 512
            oscr = nc.dram_tensor(f"oscr{j}", [SB_H, 64, S], BF16, kind="Internal").ap()
            oscr_v = oscr.rearrange("h p s -> p h s")
            oscrb = [Buf() for _ in range(NG)]
            amask = k16("amask")
            ph = ExitStack()
            with ph:
                def al(name, shape, dt):
                    return ph.enter_context(nc.sbuf_tensor(f"{name}_{j}", shape, dt))
                hTall = al("hTall", [128, DC, S], BF16)
                hTallb = [Buf() for _ in range(NT)]
                cur = load_x(src_v, 0, srcbufs[0])
                for it in range(NT):
                    xi = cur
                    if it + 1 < NT:
                        cur = load_x(src_v, it + 1, srcbufs[it + 1])
                    rms_stats(xi)
                    rms_apply(xi, "mixg", l * 8, hTall[:, :, it * TT:(it + 1) * TT], hTallb[it])
                QT = al("QT", [128, S], BF16)
                KT = al("KT", [128, S], BF16)
                V = al("V", [128, NB, 128], BF16)
                QTb = [Buf() for _ in range(NT)]
                KTb = [Buf() for _ in range(NT)]
                Vb = [Buf() for _ in range(NT)]
                e_ = [al(f"e{c}", [128, 512], F32) for c in range(2)]
                sp_ = [al(f"sp{c}", [128, 512], F32) for c in range(2)]
                lb_ = [al(f"lb{c}", [128, 512], F32) for c in range(2)]
                t_ = [al(f"t{c}", [128, 512], F32) for c in range(2)]
                spb_ = [al(f"spb{c}", [128, 512], BF16) for c in range(2)]
                Wb_ = [al(f"Wb{c}", [128, 512], BF16) for c in range(2)]
                osb_ = [al(f"osb{c}", [64, 512], BF16) for c in range(2)]
                eB = [Buf() for _ in range(2)]
                spB = [Buf() for _ in range(2)]
                lbB = [Buf() for _ in range(2)]
                tB = [Buf() for _ in range(2)]
                spbB = [Buf() for _ in range(2)]
                WbB = [Buf() for _ in range(2)]
                osbB = [Buf() for _ in range(2)]
                SBK = [[0, 1], [2, 3]]
                CBK = [4, 5]
                OBK = [6, 7]
                for hp in range(8):
                    wt, wtb = next_w(("sq", j, hp))
                    wv = v3(wt, 8, 384)
                    for it in range(NT):
                        tsl = slice(it * TT, (it + 1) * TT)
                        pi = ps_ring.next()
                        for dc in range(DC):
                            P.op("pe", lambda e, dc=dc, pi=pi, wv=wv, tsl=tsl: e.matmul(
                                psums[pi][:], wv[:, dc, 0:128], hTall[:, dc, tsl], start=(dc == 0), stop=(dc == DC - 1)),
                                reads=[wtb, hTallb[it]], writes=[psb[pi]])
                        P.op("act", lambda e, pi=pi, tsl=tsl: e.activation(out=QT[:, tsl], in_=psums[pi][:], func=AF.Copy),
                             reads=[psb[pi]], writes=[QTb[it]])
                        pi = ps_ring.next()
                        for dc in range(DC):
                            P.op("pe", lambda e, dc=dc, pi=pi, wv=wv, tsl=tsl: e.matmul(
                                psums[pi][:], wv[:, dc, 128:256], hTall[:, dc, tsl], start=(dc == 0), stop=(dc == DC - 1)),
                                reads=[wtb, hTallb[it]], writes=[psb[pi]])
                        P.op("dve", lambda e, pi=pi, tsl=tsl: e.tensor_copy(out=KT[:, tsl], in_=psums[pi][:]),
                             reads=[psb[pi]], writes=[KTb[it]])
                        pi = ps_ring.next()
                        for blk in range(4):
                            t0 = it * TT + blk * 128
                            for dc in range(DC):
                                P.op("pe", lambda e, dc=dc, pi=pi, wv=wv, t0=t0, blk=blk: e.matmul(
                                    psums[pi][:, blk * 128:(blk + 1) * 128], hTall[:, dc, t0:t0 + 128], wv[:, dc, 256:384],
                                    start=(dc == 0), stop=(dc == DC - 1)),
                                    reads=[wtb, hTallb[it]], writes=[psb[pi]])
                        P.op("act", lambda e, pi=pi, it=it: e.activation(
                            out=V[:, 4 * it:4 * it + 4, :], in_=psums[pi][:].rearrange("p (b n) -> p b n", b=4), func=AF.Copy),
                            reads=[psb[pi]], writes=[Vb[it]])
                    stepno = 0
                    for g in range(NG):
                        qsl = slice(g * 512, (g + 1) * 512)
                        for jb in range(4 * g + 3, -1, -1):
                            first = jb == 4 * g + 3
                            last = jb == 0
                            r = jb - 4 * g
                            ksl = slice(jb * 128, (jb + 1) * 128)
                            sbk = [SBK[c][stepno % 2] for c in range(2)]
                            stepno += 1
                            for c in range(2):
                                lo = 64 * c
                                P.op("pe", lambda e, c=c, lo=lo, ksl=ksl, qsl=qsl, r=r, sbk=sbk: e.matmul(
                                    psums[sbk[c]][:], KT[lo:lo + 64, ksl], QT[lo:lo + 64, qsl], start=True, stop=(r < 0)),
                                    reads=[KTb[jb // 4], QTb[g]], writes=[psb[sbk[c]]])
                                if r >= 0:
                                    P.op("pe", lambda e, c=c, r=r, sbk=sbk: e.matmul(
                                        psums[sbk[c]][:], k16("identb"), amask[:, r * 512:(r + 1) * 512], start=False, stop=True),
                                        reads=[cbuf], writes=[psb[sbk[c]]])
                            for c in range(2):
                                P.op("act", lambda e, c=c, sbk=sbk: e.activation(
                                    out=e_[c][:], in_=psums[sbk[c]][:], func=AF.Exp, scale=SB_DH ** -0.5),
                                    reads=[psb[sbk[c]]], writes=[eB[c]])
                            for c in range(2):
                                P.op("act", lambda e, c=c: e.activation(
                                    out=sp_[c][:], in_=e_[c][:], func=AF.Ln, bias=sm("one")),
                                    reads=[eB[c], cbuf], writes=[spB[c]])
                            for c in range(2):
                                P.op("pool", lambda e, c=c: e.tensor_copy(out=spb_[c][:], in_=sp_[c][:]),
                                     reads=[spB[c]], writes=[spbB[c]])
                            for c in range(2):
                                P.op("dve", lambda e, c=c, sbk=sbk: e.scalar_tensor_tensor(
                                    out=lb_[c][:], in0=psums[sbk[c]][:], scalar=SB_DH ** -0.5, in1=sp_[c][:],
                                    op0=ALU.mult, op1=ALU.subtract),
                                    reads=[psb[sbk[c]], spB[c]], writes=[lbB[c]])
                            for c in range(2):
                                P.op("pe", lambda e, c=c, first=first: e.matmul(
                                    psums[CBK[c]][:], k16("negU"), spb_[c][:], start=first, stop=True, skip_group_check=True),
                                    reads=[spbB[c], cbuf], writes=[psb[CBK[c]]])
                            for c in range(2):
                                P.op("dve", lambda e, c=c: e.tensor_tensor(
                                    out=t_[c][:], in0=psums[CBK[c]][:], in1=lb_[c][:], op=ALU.add),
                                    reads=[psb[CBK[c]], lbB[c]], writes=[tB[c]])
                            if not last:
                                for c in range(2):
                                    P.op("pe", lambda e, c=c: e.matmul(
                                        psums[CBK[c]][:], k16("negL"), spb_[c][:], start=False, stop=True, skip_group_check=True),
                                        reads=[spbB[c], cbuf], writes=[psb[CBK[c]]])
                            for c in range(2):
                                P.op("act", lambda e, c=c: e.activation(out=Wb_[c][:], in_=t_[c][:], func=AF.Exp),
                                     reads=[tB[c]], writes=[WbB[c]])
                            for c in range(2):
                                P.op("pe", lambda e, c=c, jb=jb, first=first, last=last: e.matmul(
                                    psums[OBK[c]][0:64, :], V[:, jb, 64 * c:64 * c + 64], Wb_[c][:], start=first, stop=last),
                                    reads=[Vb[jb // 4], WbB[c]], writes=[psb[OBK[c]]])
                        for c in range(2):
                            P.op("dve", lambda e, c=c: e.tensor_copy(out=osb_[c][:], in_=psums[OBK[c]][0:64, :]),
                                 reads=[psb[OBK[c]]], writes=[osbB[c]])
                            P.dma("sp", lambda e, c=c, hp=hp, qsl=qsl: e.dma_start(out=oscr[2 * hp + c][:, qsl], in_=osb_[c][:]),
                                  f"ov{c}", reads=[osbB[c]], writes=[oscrb[g]])
                P.barrier()
            ph = ExitStack()
            with ph:
                ots = [ph.enter_context(nc.sbuf_tensor(f"otile{j}_{i}", [64, SB_H, TT], BF16)) for i in range(2)]
                otB = [Buf() for _ in range(2)]
                cur = load_x(src_v, 0, srcbufs[0])
                for it in range(NT):
                    xi = cur
                    if it + 1 < NT:
                        cur = load_x(src_v, it + 1, srcbufs[it + 1])
                    oi = it % 2
                    P.dma("sp", lambda e, oi=oi, it=it: e.dma_start(out=ots[oi][:], in_=oscr_v[:, :, it * TT:(it + 1) * TT]),
                          f"ol{oi}", reads=[oscrb[it]], writes=[otB[oi]])
                    xt = xts[xi]
                    for jn in range(8):
                        wt, wtb = next_w(("so", j, jn))
                        wv = wt[0:64, 0:2048].rearrange("p (h n) -> p h n", h=SB_H)
                        pi = ps_ring2.next()
                        for h in range(SB_H):
                            P.op("pe", lambda e, h=h, pi=pi, wv=wv, oi=oi: e.matmul(
                                psums[pi][:], wv[:, h, :], ots[oi][:, h, :], start=(h == 0), stop=(h == SB_H - 1)),
                                reads=[wtb, otB[oi]], writes=[psb[pi]])
                        P.op("dve", lambda e, jn=jn, pi=pi, xt=xt: e.tensor_tensor(
                            out=xt[:, jn, :], in0=psums[pi][:], in1=xt[:, jn, :], op=ALU.add),
                            reads=[psb[pi], xtb[xi]], writes=[xtb[xi]])
                    store_x(dst_v, it, xi, dstbufs[it])
                P.barrier()

        def ssd_phase(j, l, src_v, srcbufs, dst_v, dstbufs):
            ph = ExitStack()
            with ph:
                def al(name, shape, dt):
                    return ph.enter_context(nc.sbuf_tensor(f"{name}_s{j}", shape, dt))
                hT = al("hT", [128, DC, TT], BF16)
                hTb = Buf()
                state = al("state", [128, 2048], F32)
                stateb = Buf()
                sbf = al("sbf", [128, 2048], BF16)
                sbfb = Buf()
                tails = al("tails", [128, 32, 3], F32)
                tailb = Buf()
                negA = al("negA", [128, 32], F32)
                negAb = Buf()
                ubs = [al(f"ub{i}", [128, TT + 3], F32) for i in range(2)]
                ubb = [Buf() for _ in range(2)]
                accs = [al(f"acc{i}", [128, TT], F32) for i in range(2)]
                accb = [Buf() for _ in range(2)]
                cs = al("cs", [128, TT], F32)
                csb = Buf()
                xs_tok = al("xs_tok", [128, 4, 2048], BF16)
                xs_tokb = [Buf() for _ in range(16)]
                B_tok = al("B_tok", [128, 4, 1024], BF16)
                B_tokb = [Buf() for _ in range(8)]
                BT = al("BT", [128, 8, TT], BF16)
                BTb = [Buf() for _ in range(8)]
                CT = al("CT", [128, 8, TT], BF16)
                CTb = [Buf() for _ in range(8)]
                sz = al("sz", [128, 4, 2048], BF16)
                szb = [Buf() for _ in range(4)]
                dtx = al("dtx", [128, 128], F32)
                dtxb = Buf()
                dtt = al("dtt", [128, 128], F32)
                dttb = Buf()
                at = al("at", [128, 128], F32)
                atb = Buf()
                acs = al("acs", [128, 64], F32)
                acsb = Buf()
                eall = al("eall", [128, 64], F32)
                eallb = Buf()
                dlt = al("dlt", [128, 32], F32)
                dltb = Buf()
                wsc = al("wsc", [128, 32], F32)
                wscb = Buf()
                decs = [al(f"dec{i}", [128, 512], F32) for i in range(2)]
                decb = [Buf() for _ in range(2)]
                cbms = [al(f"cbm{i}", [128, 128], F32) for i in range(2)]
                cbmb = [Buf() for _ in range(2)]
                scs = [al(f"sc{i}", [128, 512], BF16) for i in range(2)]
                scb_ = [Buf() for _ in range(2)]
                xdt = al("xdt", [128, 2048], BF16)
                xdtb = Buf()
                xw = al("xw", [128, 2048], BF16)
                xwb = Buf()
                xsD = al("xsD", [128, 2048], BF16)
                xsDb = Buf()
                yscs = [al(f"ysc{i}", [128, 256], F32) for i in range(2)]
                yscb = [Buf() for _ in range(2)]
                y = al("y", [128, 2048], F32)
                yb = Buf()
                ysq = al("ysq", [128, 2048], BF16)
                ysqb = Buf()
                ss = al("ss", [128, 1], F32)
                lnss = al("lnss", [128, 1], F32)
                rs = al("rs", [128, 1], F32)
                ssb, lnssb, rsb = Buf(), Buf(), Buf()
                yTc = al("yTc", [128, 16, 128], BF16)
                yTcb = Buf()
                LA = xts[1][:].rearrange("p a (b c) -> p (a b) c", c=128)
                LAb = xtb[1]
                ident32 = k32("ident32")
                triR = k32("triR")
                rr = Ring([0, 1])
                rr2 = Ring([0, 1])
                rr3 = Ring([0, 1])

                P.op("pool", lambda e: e.memset(tails[:], 0.0), writes=[tailb])
                P.op("pool", lambda e: e.memset(state[:], 0.0), writes=[stateb])
                P.op("pool", lambda e: e.memset(sbf[:], 0.0), writes=[sbfb])
                P.op("act", lambda e: e.activation(out=negA[:], in_=sm("alog", j * 32, 32), func=AF.Exp),
                     reads=[cbuf], writes=[negAb])
                P.op("dve", lambda e: e.tensor_scalar(out=negA[:], in0=negA[:], scalar1=-1.0, scalar2=None, op0=ALU.mult),
                     reads=[negAb], writes=[negAb])

                for it in range(NT):
                    xi = load_x(src_v, it, srcbufs[it], slot=0)
                    xt = xts[xi]
                    rms_stats(xi)
                    rms_apply(xi, "mixg", l * 8, hT, hTb)
                    for xg in range(8):
                        wt, wtb = next_w(("sx", j, xg))
                        wv = v3(wt, 8, 512)
                        for c in range(4):
                            ch = 4 * xg + c
                            pi = ps_ring.next()
                            for dc in range(DC):
                                P.op("pe", lambda e, dc=dc, c=c, pi=pi, wv=wv: e.matmul(
                                    psums[pi][:], wv[:, dc, c * 128:(c + 1) * 128], hT[:, dc, :],
                                    start=(dc == 0), stop=(dc == DC - 1)),
                                    reads=[wtb, hTb], writes=[psb[pi]])
                            ui = rr.next()
                            ub = ubs[ui]
                            acc = accs[ui]
                            P.op("dve", lambda e, ub=ub, ch=ch: e.tensor_copy(out=ub[:, 0:3], in_=tails[:, ch, :]),
                                 reads=[tailb], writes=[ubb[ui]])
                            P.op("act", lambda e, ub=ub, pi=pi: e.activation(out=ub[:, 3:TT + 3], in_=psums[pi][:], func=AF.Copy),
                                 reads=[psb[pi]], writes=[ubb[ui]])
                            P.op("dve", lambda e, ub=ub, ch=ch: e.tensor_copy(out=tails[:, ch, :], in_=ub[:, TT:TT + 3]),
                                 reads=[ubb[ui]], writes=[tailb])
                            wk = [sm("scw", (j * 4 + k) * 32 + ch) for k in range(4)]
                            bb = sm("scb", j * 32 + ch)
                            P.op("pool", lambda e, ub=ub, acc=acc, wk=wk, bb=bb: e.tensor_scalar(
                                out=acc[:], in0=ub[:, 3:TT + 3], scalar1=wk[3], scalar2=bb, op0=ALU.mult, op1=ALU.add),
                                reads=[ubb[ui], cbuf], writes=[accb[ui]])
                            for k in range(3):
                                P.op("dve", lambda e, ub=ub, acc=acc, wk=wk, k=k: e.scalar_tensor_tensor(
                                    out=acc[:], in0=ub[:, k:TT + k], scalar=wk[k], in1=acc[:], op0=ALU.mult, op1=ALU.add),
                                    reads=[ubb[ui], accb[ui], cbuf], writes=[accb[ui]])
                            if ch < 16 or ch < 24:
                                P.op("act", lambda e, acc=acc: e.activation(out=cs[:], in_=acc[:], func=AF.Silu),
                                     reads=[accb[ui]], writes=[csb])
                                if ch >= 16:
                                    g = ch - 16
                                    P.op("pool", lambda e, g=g: e.tensor_copy(out=BT[:, g, :], in_=cs[:]),
                                         reads=[csb], writes=[BTb[g]])
                                pt = ps_ring.next()
                                for blk in range(4):
                                    P.op("pe", lambda e, blk=blk, pt=pt: e.transpose(
                                        psums[pt][:, blk * 128:(blk + 1) * 128], cs[:, blk * 128:(blk + 1) * 128], ident32),
                                        reads=[csb, cbuf], writes=[psb[pt]])
                                pv = psums[pt][:].rearrange("p (b n) -> p b n", b=4)
                                if ch < 16:
                                    P.op("dve", lambda e, pv=pv, ch=ch: e.tensor_copy(
                                        out=xs_tok[:, :, ch * 128:(ch + 1) * 128], in_=pv),
                                        reads=[psb[pt]], writes=[xs_tokb[ch]])
                                else:
                                    g = ch - 16
                                    P.op("dve", lambda e, pv=pv, g=g: e.tensor_copy(
                                        out=B_tok[:, :, g * 128:(g + 1) * 128], in_=pv),
                                        reads=[psb[pt]], writes=[B_tokb[g]])
                            else:
                                g = ch - 24
                                P.op("act", lambda e, acc=acc, g=g: e.activation(out=CT[:, g, :], in_=acc[:], func=AF.Silu),
                                     reads=[accb[ui]], writes=[CTb[g]])
                    for zg in range(4):
                        wt, wtb = next_w(("sz", j, zg))
                        wv = v3(wt, 8, 512)
                        for blk in range(4):
                            pi = ps_ring.next()
                            for dc in range(DC):
                                P.op("pe", lambda e, dc=dc, blk=blk, pi=pi, wv=wv: e.matmul(
                                    psums[pi][:], hT[:, dc, blk * 128:(blk + 1) * 128], wv[:, dc, :],
                                    start=(dc == 0), stop=(dc == DC - 1)),
                                    reads=[wtb, hTb], writes=[psb[pi]])
                            P.op("act", lambda e, blk=blk, zg=zg, pi=pi: e.activation(
                                out=sz[:, blk, zg * 512:(zg + 1) * 512], in_=psums[pi][:], func=AF.Silu),
                                reads=[psb[pi]], writes=[szb[blk]])
                    wt, wtb = next_w(("sdt", j))
                    wv = v3(wt, 8, 32)
                    pi = ps_ring.next()
                    for blk in range(4):
                        for dc in range(DC):
                            P.op("pe", lambda e, dc=dc, blk=blk, pi=pi, wv=wv: e.matmul(
                                psums[pi][:, blk * 32:(blk + 1) * 32], hT[:, dc, blk * 128:(blk + 1) * 128], wv[:, dc, :],
                                start=(dc == 0), stop=(dc == DC - 1)),
                                reads=[wtb, hTb], writes=[psb[pi]])
                    dtb_bc = sm("dtb", j * 32, 32).unsqueeze(1).to_broadcast([128, 4, 32])
                    P.op("dve", lambda e, pi=pi: e.tensor_tensor(
                        out=dtx[:].rearrange("p (b h) -> p b h", b=4), in0=psums[pi][:, 0:128].rearrange("p (b h) -> p b h", b=4),
                        in1=dtb_bc, op=ALU.add),
                        reads=[psb[pi], cbuf], writes=[dtxb])
                    P.op("act", lambda e: e.activation(out=dtx[:], in_=dtx[:], func=AF.Exp), reads=[dtxb], writes=[dtxb])
                    P.op("act", lambda e: e.activation(out=dtt[:], in_=dtx[:], func=AF.Ln, bias=sm("one")),
                         reads=[dtxb, cbuf], writes=[dttb])
                    P.op("dve", lambda e: e.tensor_tensor(
                        out=at[:].rearrange("p (b h) -> p b h", b=4), in0=dtt[:].rearrange("p (b h) -> p b h", b=4),
                        in1=negA[:].unsqueeze(1).to_broadcast([128, 4, 32]), op=ALU.mult),
                        reads=[dttb, negAb], writes=[atb])
                    for blk in range(4):
                        csl = slice(blk * 128, (blk + 1) * 128)
                        a_blk = at[:, blk * 32:(blk + 1) * 32]
                        dt_blk = dtt[:, blk * 32:(blk + 1) * 32]
                        pst = ps_ring.next()
                        P.op("pe", lambda e, pst=pst, a_blk=a_blk: e.matmul(psums[pst][:, 0:32], triR, a_blk, start=True, stop=True),
                             reads=[atb, cbuf], writes=[psb[pst]])
                        P.op("pe", lambda e, pst=pst, a_blk=a_blk: e.matmul(psums[pst][:, 32:64], k32("ones32"), a_blk, start=True, stop=True),
                             reads=[atb, cbuf], writes=[psb[pst]])
                        P.op("act", lambda e, pst=pst: e.activation(out=acs[:], in_=psums[pst][:, 0:64], func=AF.Copy),
                             reads=[psb[pst]], writes=[acsb])
                        P.op("act", lambda e: e.activation(out=eall[:], in_=acs[:], func=AF.Exp), reads=[acsb], writes=[eallb])
                        P.op("dve", lambda e: e.tensor_tensor(out=dlt[:], in0=acs[:, 32:64], in1=acs[:, 0:32], op=ALU.subtract),
                             reads=[acsb], writes=[dltb])
                        P.op("act", lambda e: e.activation(out=wsc[:], in_=dlt[:], func=AF.Exp), reads=[dltb], writes=[wscb])
                        P.op("pool", lambda e, a_blk=a_blk: e.tensor_tensor(
                            out=LA, in0=k32("ustr").unsqueeze(1).to_broadcast([128, 32, 128]),
                            in1=a_blk.unsqueeze(2).to_broadcast([128, 32, 128]), op=ALU.mult),
                            reads=[atb, cbuf], writes=[LAb])
                        xs_v = xs_tok[:, blk, :].rearrange("p (h d) -> p h d", h=32)
                        P.op("pool", lambda e, xs_v=xs_v, dt_blk=dt_blk: e.tensor_tensor(
                            out=xdt[:].rearrange("p (h d) -> p h d", h=32), in0=xs_v,
                            in1=dt_blk.unsqueeze(2).to_broadcast([128, 32, 64]), op=ALU.mult),
                            reads=xs_tokb + [dttb], writes=[xdtb])
                        P.op("dve", lambda e: e.tensor_tensor(
                            out=xw[:].rearrange("p (h d) -> p h d", h=32), in0=xdt[:].rearrange("p (h d) -> p h d", h=32),
                            in1=wsc[:].unsqueeze(2).to_broadcast([128, 32, 64]), op=ALU.mult),
                            reads=[xdtb, wscb], writes=[xwb])
                        P.op("pool", lambda e, xs_v=xs_v: e.tensor_tensor(
                            out=xsD[:].rearrange("p (h d) -> p h d", h=32), in0=xs_v,
                            in1=sm("dsk", j * 32, 32).unsqueeze(2).to_broadcast([128, 32, 64]), op=ALU.mult),
                            reads=xs_tokb + [cbuf], writes=[xsDb])
                        for g in range(8):
                            pg = ps_ring.next()
                            for hh in range(4):
                                P.op("pe", lambda e, pg=pg, hh=hh, g=g: e.matmul(
                                    psums[pg][:, hh * 128:(hh + 1) * 128], LA[:, 4 * g + hh, :], triR, start=True, stop=True),
                                    reads=[LAb, cbuf], writes=[psb[pg]])
                            di = rr2.next()
                            P.op("act", lambda e, pg=pg, di=di: e.activation(out=decs[di][:], in_=psums[pg][:], func=AF.Exp),
                                 reads=[psb[pg]], writes=[decb[di]])
                            pc = ps_ring.next()
                            P.op("pe", lambda e, pc=pc, g=g, csl=csl: e.matmul(
                                psums[pc][:, 0:128], BT[:, g, csl], CT[:, g, csl], start=True, stop=True),
                                reads=[BTb[g], CTb[g]], writes=[psb[pc]])
                            P.op("dve", lambda e, pc=pc, di=di: e.tensor_tensor(
                                out=cbms[di][:], in0=psums[pc][:, 0:128], in1=k32("cmask"), op=ALU.mult),
                                reads=[psb[pc], cbuf], writes=[cbmb[di]])
                            P.op("pool", lambda e, di=di: e.tensor_tensor(
                                out=scs[di][:].rearrange("p (h t) -> p h t", h=4), in0=decs[di][:].rearrange("p (h t) -> p h t", h=4),
                                in1=cbms[di][:].unsqueeze(1).to_broadcast([128, 4, 128]), op=ALU.mult),
                                reads=[decb[di], cbmb[di]], writes=[scb_[di]])
                            py = ps_ring2.next()
                            for hh in range(4):
                                h = 4 * g + hh
                                P.op("pe", lambda e, py=py, hh=hh, h=h, di=di: e.matmul(
                                    psums[py][:, hh * 64:(hh + 1) * 64], scs[di][:, hh * 128:(hh + 1) * 128],
                                    xdt[:, h * 64:(h + 1) * 64], start=(hh == 0), stop=False, skip_group_check=True),
                                    reads=[scb_[di], xdtb], writes=[psb[py]])
                            P.op("pe", lambda e, py=py, g=g: e.matmul(
                                psums[py][:, 0:256], k16("identb"), xsD[:, g * 256:(g + 1) * 256], start=False, stop=True,
                                skip_group_check=True),
                                reads=[xsDb, cbuf], writes=[psb[py]])
                            P.op("pe", lambda e, py=py, g=g, csl=csl: e.matmul(
                                psums[py][:, 256:512], CT[:, g, csl], sbf[:, g * 256:(g + 1) * 256], start=False, stop=True,
                                skip_group_check=True),
                                reads=[CTb[g], sbfb], writes=[psb[py]])
                            yi = rr3.next()
                            P.op("dve", lambda e, py=py, g=g, yi=yi: e.tensor_tensor(
                                out=yscs[yi][:].rearrange("p (h d) -> p h d", h=4),
                                in0=psums[py][:, 256:512].rearrange("p (h d) -> p h d", h=4),
                                in1=eall[:, 4 * g:4 * g + 4].unsqueeze(2).to_broadcast([128, 4, 64]), op=ALU.mult),
                                reads=[psb[py], eallb], writes=[yscb[yi]])
                            P.op("dve", lambda e, py=py, g=g, yi=yi: e.tensor_tensor(
                                out=y[:, g * 256:(g + 1) * 256], in0=psums[py][:, 0:256], in1=yscs[yi][:], op=ALU.add),
                                reads=[psb[py], yscb[yi]], writes=[yb])
                        P.op("pool", lambda e: e.tensor_tensor(
                            out=state[:].rearrange("p (h d) -> p h d", h=32), in0=state[:].rearrange("p (h d) -> p h d", h=32),
                            in1=eall[:, 32:64].unsqueeze(2).to_broadcast([128, 32, 64]), op=ALU.mult),
                            reads=[stateb, eallb], writes=[stateb])
                        for g2 in range(4):
                            pu = ps_ring.next()
                            for q in range(2):
                                g = 2 * g2 + q
                                P.op("pe", lambda e, pu=pu, q=q, g=g, blk=blk: e.matmul(
                                    psums[pu][:, q * 256:(q + 1) * 256], B_tok[:, blk, g * 128:(g + 1) * 128],
                                    xw[:, g * 256:(g + 1) * 256], start=True, stop=True),
                                    reads=[B_tokb[g], xwb], writes=[psb[pu]])
                            P.op("dve", lambda e, pu=pu, g2=g2: e.tensor_tensor(
                                out=state[:, g2 * 512:(g2 + 1) * 512], in0=psums[pu][:], in1=state[:, g2 * 512:(g2 + 1) * 512], op=ALU.add),
                                reads=[psb[pu], stateb], writes=[stateb])
                        P.op("pool", lambda e: e.tensor_copy(out=sbf[:], in_=state[:]), reads=[stateb], writes=[sbfb])
                        P.op("pool", lambda e, blk=blk: e.tensor_tensor(out=y[:], in0=y[:], in1=sz[:, blk, :], op=ALU.mult),
                             reads=[yb, szb[blk]], writes=[yb])
                        P.op("act", lambda e: e.activation(out=ysq[:], in_=y[:], func=AF.Square, accum_out=ss[:]),
                             reads=[yb], writes=[ysqb, ssb])
                        P.op("act", lambda e: e.activation(out=lnss[:], in_=ss[:], func=AF.Ln, scale=1.0 / SSD_DI, bias=sm("eps")),
                             reads=[ssb, cbuf], writes=[lnssb])
                        P.op("act", lambda e: e.activation(out=rs[:], in_=lnss[:], func=AF.Exp, scale=-0.5),
                             reads=[lnssb], writes=[rsb])
                        P.op("dve", lambda e: e.tensor_scalar(out=y[:], in0=y[:], scalar1=rs[:, 0:1], scalar2=None, op0=ALU.mult),
                             reads=[yb, rsb], writes=[yb])
                        for c4 in range(4):
                            pt = ps_ring.next()
                            for q in range(4):
                                cc = 4 * c4 + q
                                P.op("pe", lambda e, pt=pt, q=q, cc=cc: e.transpose(
                                    psums[pt][:, q * 128:(q + 1) * 128], y[:, cc * 128:(cc + 1) * 128], ident32),
                                    reads=[yb, cbuf], writes=[psb[pt]])
                            for q in range(4):
                                cc = 4 * c4 + q
                                P.op("act" if q % 2 == 0 else "dve", (lambda e, pt=pt, q=q, cc=cc: e.activation(
                                    out=yTc[:, cc, :], in_=psums[pt][:, q * 128:(q + 1) * 128], func=AF.Identity,
                                    scale=sm("sng", j * 16 + cc))) if q % 2 == 0 else (lambda e, pt=pt, q=q, cc=cc: e.tensor_scalar(
                                    out=yTc[:, cc, :], in0=psums[pt][:, q * 128:(q + 1) * 128], scalar1=sm("sng", j * 16 + cc),
                                    scalar2=None, op0=ALU.mult)),
                                    reads=[psb[pt], cbuf], writes=[yTcb])
                        for k in range(4):
                            wt, wtb = next_w(("sout", j, k))
                            wv = v3(wt, 16, 256)
                            for nl in range(2):
                                pi = ps_ring2.next()
                                for cc in range(16):
                                    P.op("pe", lambda e, pi=pi, cc=cc, nl=nl, wv=wv: e.matmul(
                                        psums[pi][:, 0:128], wv[:, cc, nl * 128:(nl + 1) * 128], yTc[:, cc, :],
                                        start=(cc == 0), stop=(cc == 15)),
                                        reads=[wtb, yTcb], writes=[psb[pi]])
                                P.op("dve", lambda e, pi=pi, k=k, nl=nl, csl=csl, xt=xt: e.tensor_tensor(
                                    out=xt[:, 2 * k + nl, csl], in0=psums[pi][:, 0:128], in1=xt[:, 2 * k + nl, csl], op=ALU.add),
                                    reads=[psb[pi], xtb[xi]], writes=[xtb[xi]])
                    store_x(dst_v, it, xi, dstbufs[it])
                P.barrier()

        def final_phase(src_v, srcbufs):
            ph = ExitStack()
            with ph:
                ots = [ph.enter_context(nc.sbuf_tensor(f"ot{i}", [128, DC, TT], F32)) for i in range(2)]
                otb = [Buf() for _ in range(2)]
                cur = load_x(src_v, 0, srcbufs[0])
                for it in range(NT):
                    xi = cur
                    if it + 1 < NT:
                        cur = load_x(src_v, it + 1, srcbufs[it + 1])
                    rms_stats(xi)
                    o = it % 2
                    rms_apply(xi, "fing", 0, ots[o], otb[o])
                    P.dma("pool", lambda e, o=o, it=it: e.dma_start(out=out_v[:, :, it * TT:(it + 1) * TT], in_=ots[o][:]),
                          f"os{o}", reads=[otb[o]], writes=[outb])
                P.barrier()

        outb = Buf("out")
        src_v, srcb = (xin_v, [None] * NT)
        for item in plan:
            if item[0] == "ffn":
                ffn_phase(item[1], src_v, srcb, xres_v, xres_b)
            if item[0] == "sb":
                sb_phase(item[1], item[2], src_v, srcb, xres_v, xres_b)
            if item[0] == "ssd":
                ssd_phase(item[1], item[2], src_v, srcb, xres_v, xres_b)
            src_v, srcb = xres_v, xres_b
        if do_final:
            final_phase(src_v, srcb)
        else:
            cur = None
            for it in range(NT):
                xi = load_x(src_v, it, srcb[it])
                P.dma("pool", lambda e, xi=xi, it=it: e.dma_start(out=out_v[:, :, it * TT:(it + 1) * TT], in_=xts[xi][:]),
                      f"os{xi}", reads=[xtb[xi]], writes=[outb])
            P.barrier()
        P.barrier()
        P.emit()
    return nc


def prep_inputs(inp):
    small, scols = pack_small(inp)
    eps = np.full((128, 1), NORM_EPS, np.float32)
    scols["eps"] = (small.shape[1], 1)
    scols["one"] = (small.shape[1] + 1, 1)
    small = np.concatenate([small, eps, np.ones((128, 1), np.float32)], axis=1)
    c16, c16cols, c32, c32cols = make_consts()
    return small, scols, c16, c16cols, c32, c32cols


FULL_PLAN = [("ssd", 0, 0), ("ffn", 0), ("sb", 0, 1), ("ffn", 1), ("ssd", 1, 2), ("ffn", 2), ("sb", 1, 3), ("ffn", 3)]


def run_plan(inp, plan, S, x_batch, do_final=True, ncores=8, trace=False):
    small, scols, c16, c16cols, c32, c32cols = prep_inputs(inp)
    nc = build_program(S, plan, scols, small.shape[1], c16cols, c16.shape[1], c32cols, c32.shape[1], do_final=do_final)
    kinds = set(p[0] for p in plan)
    shared = {"small": small, "c16": c16, "c32": c32}
    if "ffn" in kinds:
        shared["ffn_w_in"] = np.ascontiguousarray(inp["ffn_w_in"], np.float32)
        shared["ffn_w_out"] = np.ascontiguousarray(inp["ffn_w_out"], np.float32)
    if "ssd" in kinds:
        shared["ssd_w_in"] = np.ascontiguousarray(inp["ssd_w_in"], np.float32)
        shared["ssd_w_out"] = np.ascontiguousarray(inp["ssd_w_out"], np.float32)
    if "sb" in kinds:
        shared["sb_w_qkv"] = np.ascontiguousarray(inp["sb_w_qkv"], np.float32)
        shared["sb_w_out"] = np.ascontiguousarray(inp["sb_w_out"], np.float32)
    in_maps = []
    for b in range(ncores):
        m = dict(shared)
        m["xT"] = np.ascontiguousarray(x_batch[b].T)
        in_maps.append(m)
    res = run_bass_kernel_spmd(nc, in_maps, core_ids=list(range(ncores)), trace=trace)
    out = np.stack([np.ascontiguousarray(r["outT"].T) for r in res.results], axis=0)
    return out, res


def kernel(**inputs):
    x = np.asarray(inputs["x"], np.float32)
    out, _ = run_plan(inputs, FULL_PLAN, x.shape[1], x)
    return out.astype(np.float32)
```

```python
from contextlib import ExitStack
import numpy as np
import concourse.bass as bass
import concourse.mybir as mybir
from concourse.bass_utils import run_bass_kernel_spmd

F32 = mybir.dt.float32
BF16 = mybir.dt.bfloat16
AF = mybir.ActivationFunctionType
ALU = mybir.AluOpType

D = 1024
DC = 8
NORM_EPS = 1e-6
DFF = 2816
FC = DFF // 128
TT = 512
WSLOT = 4096
NWS = 3

SSD_DI = 2048
SSD_H = 32
SSD_P = 64
SSD_G = 8
SSD_N = 128
SSD_CONVD = 4096
SSD_IN = 6176
SB_H = 16
SB_DH = 64
NEG = -30000.0


class Buf:
    __slots__ = ("w", "r", "name")

    def __init__(self, name=""):
        self.w = None
        self.r = {}
        self.name = name


class Prog:
    ENGS = ("pe", "act", "dve", "pool", "sp")

    def __init__(self, nc, stack):
        self.nc = nc
        self.stack = stack
        self.streams = {e: [] for e in self.ENGS}
        self.sem = {e: stack.enter_context(nc.semaphore("s_" + e)) for e in self.ENGS}
        self.cnt = {e: 0 for e in self.ENGS}
        self.waited = {e: {} for e in self.ENGS}
        self.dmasems = {}
        self.dmacnt = {}
        self.same_engine_sync = True

    def _waits(self, eng, reads, writes):
        deps = {}

        def add(ev):
            if ev is None:
                return
            k, v = ev
            if deps.get(k, 0) < v:
                deps[k] = v

        for b in reads:
            add(b.w)
        for b in writes:
            add(b.w)
            for k, v in b.r.items():
                add((k, v))
        out = []
        for k, v in deps.items():
            if k == eng and (eng == "pe" or not self.same_engine_sync):
                continue
            if self.waited[eng].get(k, 0) >= v:
                continue
            self.waited[eng][k] = v
            out.append((k, v))
        return out

    def _commit(self, ev, reads, writes):
        k, v = ev
        for b in writes:
            b.w = ev
            b.r = {}
        for b in reads:
            if b.r.get(k, 0) < v:
                b.r[k] = v

    def op(self, eng, fn, reads=(), writes=()):
        waits = self._waits(eng, reads, writes)
        self.cnt[eng] += 1
        ev = (eng, self.cnt[eng])
        self.streams[eng].append((waits, fn, eng, 1))
        self._commit(ev, reads, writes)

    def dma(self, eng, fn, semname, reads=(), writes=()):
        if semname not in self.dmasems:
            self.dmasems[semname] = self.stack.enter_context(self.nc.semaphore("d_" + semname))
            self.dmacnt[semname] = 0
        waits = self._waits(eng, reads, writes)
        self.dmacnt[semname] += 16
        key = "D:" + semname
        ev = (key, self.dmacnt[semname])
        self.streams[eng].append((waits, fn, key, 16))
        self._commit(ev, reads, writes)

    def semh(self, key):
        if key.startswith("D:"):
            return self.dmasems[key[2:]]
        return self.sem[key]

    def barrier(self):
        evs = [(e, self.cnt[e]) for e in self.ENGS if self.cnt[e] > 0]
        evs += [("D:" + s, c) for s, c in self.dmacnt.items() if c > 0]
        for e in self.ENGS:
            waits = []
            for k, v in evs:
                if k == e:
                    continue
                if self.waited[e].get(k, 0) >= v:
                    continue
                self.waited[e][k] = v
                waits.append((k, v))
            if waits:
                self.streams[e].append((waits, None, None, 0))

    def emit(self):
        nc = self.nc
        engobj = {"pe": "tensor", "act": "scalar", "dve": "vector", "pool": "gpsimd", "sp": "sync"}
        with nc.Block() as block:
            for e in self.ENGS:
                stream = self.streams[e]

                def body(eng, stream=stream):
                    for waits, fn, key, inc in stream:
                        for k, v in waits:
                            eng.wait_ge(self.semh(k), v)
                        if fn is not None:
                            fn(eng).then_inc(self.semh(key), inc)

                getattr(block, engobj[e])(body)


class Ring:
    def __init__(self, items):
        self.items = items
        self.i = 0

    def next(self):
        it = self.items[self.i % len(self.items)]
        self.i += 1
        return it


def pack_small(inp):
    cols = {}
    arrs = []
    off = 0

    def put(name, a):
        nonlocal off
        a = np.ascontiguousarray(a, dtype=np.float32)
        assert a.shape[0] == 128
        cols[name] = (off, a.shape[1])
        arrs.append(a)
        off += a.shape[1]

    def pm(v, nchunk):
        v = np.asarray(v, np.float32)
        lead = v.shape[:-1]
        v = v.reshape(lead + (nchunk, 128))
        v = np.moveaxis(v, -1, 0)
        return v.reshape(128, -1)

    def bc(v):
        v = np.asarray(v, np.float32).reshape(1, -1)
        return np.broadcast_to(v, (128, v.shape[1]))

    put("mixg", pm(inp["mix_norm"], 8))
    put("ffng", pm(inp["ffn_norm"], 8))
    put("fing", pm(inp["final_norm"], 8))
    put("fcw", pm(inp["ffn_conv_w"], 44))
    put("fcb", pm(inp["ffn_conv_b"], 44))
    put("scw", pm(inp["ssd_conv_w"], 32))
    put("scb", pm(inp["ssd_conv_b"], 32))
    put("sng", pm(inp["ssd_norm"], 16))
    put("dtb", bc(inp["ssd_dt_bias"]))
    put("alog", bc(inp["ssd_a_log"]))
    put("dsk", bc(inp["ssd_d"]))
    return np.concatenate(arrs, axis=1), cols


def make_consts():
    k = np.arange(128)
    c = {}
    ident = np.eye(128, dtype=np.float32)
    negU = -(k[:, None] >= k[None, :]).astype(np.float32)
    negL = -(k[:, None] < k[None, :]).astype(np.float32)
    q = np.arange(512)
    am = np.stack([np.where(r * 128 + k[:, None] < q[None, :], 0.0, NEG) for r in range(4)], axis=1)
    cb16 = np.concatenate([ident, np.ones((128, 128), np.float32), negU, negL, am.reshape(128, -1)], axis=1)
    c["identb"] = (0, 128)
    c["onesb"] = (128, 128)
    c["negUi"] = (256, 128)
    c["negLs"] = (384, 128)
    c["amask"] = (512, 2048)
    triR = (k[:, None] <= k[None, :]).astype(np.float32)
    ustr = (k[:, None] > k[None, :]).astype(np.float32)
    cmask = (k[:, None] <= k[None, :]).astype(np.float32)
    c32 = np.concatenate([triR, ustr, cmask, np.ones((128, 128), np.float32), ident], axis=1)
    f = {"triR": (0, 128), "ustr": (128, 128), "cmask": (256, 128), "ones32": (384, 128), "ident32": (512, 128)}
    import ml_dtypes
    return cb16.astype(ml_dtypes.bfloat16), c, c32.astype(np.float32), f


def build_program(S, plan, small_cols, n_small, cst16_cols, n_c16, cst32_cols, n_c32,
                  do_final=True, x_from_input=True):
    NT = S // TT
    nc = bass.Bass("TRN2", target_bir_lowering=False)
    xin = nc.dram_tensor("xT", [D, S], F32, kind="ExternalInput").ap()
    outT = nc.dram_tensor("outT", [D, S], F32, kind="ExternalOutput").ap()
    small_d = nc.dram_tensor("small", [128, n_small], F32, kind="ExternalInput").ap()
    c16_d = nc.dram_tensor("c16", [128, n_c16], BF16, kind="ExternalInput").ap()
    c32_d = nc.dram_tensor("c32", [128, n_c32], F32, kind="ExternalInput").ap()
    xres = nc.dram_tensor("xres", [D, S], F32, kind="Internal").ap()
    xin_v = xin.rearrange("(dc p) s -> p dc s", p=128)
    xres_v = xres.rearrange("(dc p) s -> p dc s", p=128)
    out_v = outT.rearrange("(dc p) s -> p dc s", p=128)

    kinds = set(p[0] for p in plan)
    wd = {}
    if "ffn" in kinds:
        wd["ffn_w_in"] = nc.dram_tensor("ffn_w_in", [4, D, 2 * DFF], F32, kind="ExternalInput").ap()
        wd["ffn_w_out"] = nc.dram_tensor("ffn_w_out", [4, DFF, D], F32, kind="ExternalInput").ap()
    if "ssd" in kinds:
        wd["ssd_w_in"] = nc.dram_tensor("ssd_w_in", [2, D, SSD_IN], F32, kind="ExternalInput").ap()
        wd["ssd_w_out"] = nc.dram_tensor("ssd_w_out", [2, SSD_DI, D], F32, kind="ExternalInput").ap()
    if "sb" in kinds:
        wd["sb_w_qkv"] = nc.dram_tensor("sb_w_qkv", [2, D, 3 * D], F32, kind="ExternalInput").ap()
        wd["sb_w_out"] = nc.dram_tensor("sb_w_out", [2, D, D], F32, kind="ExternalInput").ap()

    st = ExitStack()
    with st:
        P = Prog(nc, st)

        def sb(name, shape, dt):
            return st.enter_context(nc.sbuf_tensor(name, shape, dt))

        small = sb("small_sb", [128, n_small], F32)
        c16 = sb("c16_sb", [128, n_c16], BF16)
        c32 = sb("c32_sb", [128, n_c32], F32)
        wslots = [sb(f"wslot{i}", [128, WSLOT], BF16) for i in range(NWS)]
        wbuf = [Buf(f"wslot{i}") for i in range(NWS)]
        xts = [sb(f"xt{i}", [128, DC, TT], F32) for i in range(2)]
        xtb = [Buf(f"xt{i}") for i in range(2)]
        sqb = [sb(f"sq{i}", [128, TT], BF16) for i in range(2)]
        sqbuf = [Buf() for _ in range(2)]
        rstd = sb("rstd", [128, TT], F32)
        rstdb = Buf("rstd")
        lnv = sb("lnv", [128, TT], F32)
        lnvb = Buf("lnv")
        psums = [st.enter_context(nc.psum_tensor(f"ps{i}", [128, 512], F32)) for i in range(8)]
        psb = [Buf(f"ps{i}") for i in range(8)]
        cbuf = Buf("consts")
        xres_b = [Buf(f"xres{i}") for i in range(NT)]

        def sm(name, idx=0, n=1):
            o, _ = small_cols[name]
            return small[:, o + idx:o + idx + n]

        def k16(name):
            o, n = cst16_cols[name]
            return c16[:, o:o + n]

        def k32(name):
            o, n = cst32_cols[name]
            return c32[:, o:o + n]

        P.dma("sp", lambda e: e.dma_start(out=small[:], in_=small_d), "cst", writes=[cbuf])
        P.dma("sp", lambda e: e.dma_start(out=c16[:], in_=c16_d), "cst", writes=[cbuf])
        P.dma("sp", lambda e: e.dma_start(out=c32[:], in_=c32_d), "cst", writes=[cbuf])

        wgroups = {}

        def conv_group(key, pieces, n, parts=128):
            scr = nc.dram_tensor("scr_" + "_".join(str(k) for k in key), [parts, n], BF16, kind="Internal").ap()
            b = Buf("scr")
            i = conv_group.i % NWS
            conv_group.i += 1
            slot = wslots[i]
            for src, dstf in pieces:
                P.dma("pool", lambda e, src=src, dstf=dstf, slot=slot: e.dma_start(out=dstf(slot), in_=src),
                      f"cv{i}", writes=[wbuf[i]])
            P.dma("sp", lambda e, slot=slot, scr=scr, n=n: e.dma_start(out=scr, in_=slot[0:parts, 0:n]),
                  f"cs{i}", reads=[wbuf[i]], writes=[b])
            wgroups[key] = (scr, b, n, parts)

        conv_group.i = 0

        def v3(slot, a, b, lo=None, hi=None):
            v = slot[:, 0:a * b].rearrange("p (a b) -> p a b", a=a)
            if lo is not None:
                v = v[:, :, lo:hi]
            return v

        for item in plan:
            if item[0] == "ffn":
                l = item[1]
                Wv = wd["ffn_w_in"][l].rearrange("(dc p) n -> p dc n", p=128)
                for g in range(11):
                    conv_group(("fi", l, g), [
                        (Wv[:, :, 256 * g:256 * g + 256], lambda s: v3(s, 8, 512, 0, 256)),
                        (Wv[:, :, DFF + 256 * g:DFF + 256 * g + 256], lambda s: v3(s, 8, 512, 256, 512)),
                    ], 4096)
                Wo = wd["ffn_w_out"][l].rearrange("(fc p) n -> p fc n", p=128)
                for j in range(8):
                    conv_group(("fo", l, j), [(Wo[:, :, 128 * j:128 * j + 128], lambda s: v3(s, FC, 128))], FC * 128)
            if item[0] == "ssd":
                j = item[1]
                Wv = wd["ssd_w_in"][j].rearrange("(dc p) n -> p dc n", p=128)
                for zg in range(4):
                    conv_group(("sz", j, zg), [(Wv[:, :, 512 * zg:512 * zg + 512], lambda s: v3(s, 8, 512))], 4096)
                for xg in range(8):
                    conv_group(("sx", j, xg), [(Wv[:, :, 2048 + 512 * xg:2048 + 512 * xg + 512], lambda s: v3(s, 8, 512))], 4096)
                conv_group(("sdt", j), [(Wv[:, :, 6144:6176], lambda s: v3(s, 8, 32))], 256)
                Wo = wd["ssd_w_out"][j].rearrange("(cc p) n -> p cc n", p=128)
                for k in range(4):
                    conv_group(("sout", j, k), [(Wo[:, :, 256 * k:256 * k + 256], lambda s: v3(s, 16, 256))], 4096)
            if item[0] == "sb":
                j = item[1]
                Wv = wd["sb_w_qkv"][j].rearrange("(dc p) n -> p dc n", p=128)
                for hp in range(8):
                    conv_group(("sq", j, hp), [
                        (Wv[:, :, 128 * hp:128 * hp + 128], lambda s: v3(s, 8, 384, 0, 128)),
                        (Wv[:, :, D + 128 * hp:D + 128 * hp + 128], lambda s: v3(s, 8, 384, 128, 256)),
                        (Wv[:, :, 2 * D + 128 * hp:2 * D + 128 * hp + 128], lambda s: v3(s, 8, 384, 256, 384)),
                    ], 3072)
                Wo = wd["sb_w_out"][j].rearrange("(h p) n -> p h n", p=64)
                for jn in range(8):
                    conv_group(("so", j, jn), [
                        (Wo[:, :, 128 * jn:128 * jn + 128],
                         lambda s: s[0:64, 0:2048].rearrange("p (h n) -> p h n", h=16))], 2048, parts=64)

        wsched = []
        wstate = {"issued": 0, "used": 0}

        def w_issue_upto(n):
            while wstate["issued"] < min(n, len(wsched)):
                i = wstate["issued"]
                key = wsched[i]
                scr, b, ncol, parts = wgroups[key]
                s = i % NWS
                P.dma("sp", lambda e, s=s, scr=scr, ncol=ncol, parts=parts: e.dma_start(out=wslots[s][0:parts, 0:ncol], in_=scr),
                      f"wl{s}", reads=[b], writes=[wbuf[s]])
                wstate["issued"] += 1

        def next_w(key):
            i = wstate["used"]
            assert wsched[i] == key, (wsched[i], key)
            w_issue_upto(i + NWS)
            wstate["used"] += 1
            s = i % NWS
            return wslots[s], wbuf[s]

        for item in plan:
            if item[0] == "ffn":
                l = item[1]
                for it in range(NT):
                    wsched.extend(("fi", l, g) for g in range(11))
                    wsched.extend(("fo", l, j) for j in range(8))
            if item[0] == "ssd":
                j = item[1]
                for it in range(NT):
                    wsched.extend(("sx", j, xg) for xg in range(8))
                    wsched.extend(("sz", j, zg) for zg in range(4))
                    wsched.append(("sdt", j))
                    for blk in range(4):
                        wsched.extend(("sout", j, k) for k in range(4))
            if item[0] == "sb":
                j = item[1]
                wsched.extend(("sq", j, hp) for hp in range(8))
                for it in range(NT):
                    wsched.extend(("so", j, jn) for jn in range(8))

        xstate = {"n": 0}

        def load_x(src_v, it, srcbuf, slot=None):
            i = xstate["n"] % 2 if slot is None else slot
            xstate["n"] += 1
            P.dma("sp", lambda e, i=i, it=it: e.dma_start(out=xts[i][:], in_=src_v[:, :, it * TT:(it + 1) * TT]),
                  f"xl{i}", reads=[srcbuf] if srcbuf is not None else [], writes=[xtb[i]])
            return i

        def store_x(dst_v, it, i, dstbuf):
            P.dma("pool", lambda e, i=i, it=it: e.dma_start(out=dst_v[:, :, it * TT:(it + 1) * TT], in_=xts[i][:]),
                  f"xs{i}", reads=[xtb[i]], writes=[dstbuf] if dstbuf is not None else [])

        ps_ring = Ring([0, 1, 2, 3])
        ps_ring2 = Ring([4, 5])
        PS_NORM = 6

        def rms_stats(xi):
            xt = xts[xi]
            for dc in range(DC):
                q = dc % 2
                P.op("act", lambda e, dc=dc, q=q: e.activation(out=sqb[q][:], in_=xt[:, dc, :], func=AF.Square),
                     reads=[xtb[xi]], writes=[sqbuf[q]])
                P.op("pe", lambda e, dc=dc, q=q: e.matmul(psums[PS_NORM][:], k16("onesb"), sqb[q][:],
                                                          start=(dc == 0), stop=(dc == DC - 1)),
                     reads=[sqbuf[q], cbuf], writes=[psb[PS_NORM]])
            P.op("act", lambda e: e.activation(out=lnv[:], in_=psums[PS_NORM][:], func=AF.Ln,
                                               scale=1.0 / D, bias=sm("eps")),
                 reads=[psb[PS_NORM], cbuf], writes=[lnvb])
            P.op("act", lambda e: e.activation(out=rstd[:], in_=lnv[:], func=AF.Exp, scale=-0.5),
                 reads=[lnvb], writes=[rstdb])

        def rms_apply(xi, gname, gidx, out_t, out_b):
            xt = xts[xi]
            for dc in range(DC):
                P.op("dve", lambda e, dc=dc: e.scalar_tensor_tensor(
                    out=out_t[:, dc, :], in0=xt[:, dc, :], scalar=sm(gname, gidx + dc), in1=rstd[:],
                    op0=ALU.mult, op1=ALU.mult),
                    reads=[xtb[xi], rstdb, cbuf], writes=[out_b])

        def ffn_phase(l, src_v, srcbufs, dst_v, dstbufs):
            ph = ExitStack()
            with ph:
                hT = ph.enter_context(nc.sbuf_tensor(f"hT{l}", [128, DC, TT], BF16))
                hTb = Buf("hT")
                gT = ph.enter_context(nc.sbuf_tensor(f"gT{l}", [128, FC, TT], BF16))
                gTb = [Buf() for _ in range(FC)]
                tails = ph.enter_context(nc.sbuf_tensor(f"tails{l}", [128, 2 * FC, 2], F32))
                tailb = [Buf("tails") for _ in range(2 * FC)]
                ubs = [ph.enter_context(nc.sbuf_tensor(f"ub{l}_{i}", [128, TT + 2], F32)) for i in range(4)]
                ubb = [Buf() for _ in range(4)]
                accs = [ph.enter_context(nc.sbuf_tensor(f"acc{l}_{i}", [128, TT], F32)) for i in range(8)]
                accb = [Buf() for _ in range(8)]
                sgs = [ph.enter_context(nc.sbuf_tensor(f"sg{l}_{i}", [128, TT], F32)) for i in range(2)]
                sgb = [Buf() for _ in range(2)]
                ubr = Ring([0, 1])
                sgr = Ring([0, 1])
                P.op("pool", lambda e: e.memset(tails[:], 0.0), writes=tailb)
                cur = load_x(src_v, 0, srcbufs[0])
                for it in range(NT):
                    xi = cur
                    if it + 1 < NT:
                        cur = load_x(src_v, it + 1, srcbufs[it + 1])
                    rms_stats(xi)
                    rms_apply(xi, "ffng", l * 8, hT, hTb)
                    NCH = 44
                    cinfo = {}

                    def sPE(idx):
                        g, c = divmod(idx, 4)
                        if c == 0:
                            wt, wtb = next_w(("fi", l, g))
                            sPE.cur = (v3(wt, 8, 512), wtb)
                        wv, wtb = sPE.cur
                        pi = ps_ring.next()
                        cinfo[idx] = pi
                        for dc in range(DC):
                            P.op("pe", lambda e, dc=dc, c=c, pi=pi, wv=wv: e.matmul(
                                psums[pi][:], wv[:, dc, c * 128:(c + 1) * 128], hT[:, dc, :],
                                start=(dc == 0), stop=(dc == DC - 1)),
                                reads=[wtb, hTb], writes=[psb[pi]])

                    def chan(idx):
                        g, c = divmod(idx, 4)
                        return 2 * g + c if c < 2 else FC + 2 * g + (c - 2)

                    def sIn(idx):
                        ch = chan(idx)
                        pi = cinfo[idx]
                        ui = idx % 4
                        ub = ubs[ui]
                        P.op("dve", lambda e, ub=ub, ch=ch: e.tensor_copy(out=ub[:, 0:2], in_=tails[:, ch, :]),
                             reads=[tailb[ch]], writes=[ubb[ui]])
                        P.op("act", lambda e, ub=ub, pi=pi: e.activation(out=ub[:, 2:TT + 2], in_=psums[pi][:], func=AF.Copy),
                             reads=[psb[pi]], writes=[ubb[ui]])

                    def sMid(idx):
                        ch = chan(idx)
                        ui = idx % 4
                        ub = ubs[ui]
                        ai = idx % 8
                        acc = accs[ai]
                        P.op("dve", lambda e, ub=ub, ch=ch: e.tensor_copy(out=tails[:, ch, :], in_=ub[:, TT:TT + 2]),
                             reads=[ubb[ui]], writes=[tailb[ch]])
                        w2 = sm("fcw", (l * 3 + 2) * 44 + ch)
                        bb = sm("fcb", l * 44 + ch)
                        P.op("pool", lambda e, ub=ub, acc=acc, w2=w2, bb=bb: e.tensor_scalar(
                            out=acc[:], in0=ub[:, 2:TT + 2], scalar1=w2, scalar2=bb, op0=ALU.mult, op1=ALU.add),
                            reads=[ubb[ui], cbuf], writes=[accb[ai]])

                    def sSTT(idx):
                        ch = chan(idx)
                        ui = idx % 4
                        ub = ubs[ui]
                        ai = idx % 8
                        acc = accs[ai]
                        w0 = sm("fcw", (l * 3 + 0) * 44 + ch)
                        w1 = sm("fcw", (l * 3 + 1) * 44 + ch)
                        P.op("dve", lambda e, ub=ub, acc=acc, w1=w1: e.scalar_tensor_tensor(
                            out=acc[:], in0=ub[:, 1:TT + 1], scalar=w1, in1=acc[:], op0=ALU.mult, op1=ALU.add),
                            reads=[ubb[ui], accb[ai], cbuf], writes=[accb[ai]])
                        P.op("dve", lambda e, ub=ub, acc=acc, w0=w0: e.scalar_tensor_tensor(
                            out=acc[:], in0=ub[:, 0:TT], scalar=w0, in1=acc[:], op0=ALU.mult, op1=ALU.add),
                            reads=[ubb[ui], accb[ai], cbuf], writes=[accb[ai]])

                    def sGate(g):
                        for c in range(2):
                            si = sgr.next()
                            fi = 2 * g + c
                            ga = (4 * g + c) % 8
                            ua = (4 * g + 2 + c) % 8
                            P.op("act", lambda e, si=si, ga=ga: e.activation(out=sgs[si][:], in_=accs[ga][:], func=AF.Silu),
                                 reads=[accb[ga]], writes=[sgb[si]])
                            P.op("pool", lambda e, si=si, ua=ua, fi=fi: e.tensor_tensor(
                                out=gT[:, fi, :], in0=sgs[si][:], in1=accs[ua][:], op=ALU.mult),
                                reads=[sgb[si], accb[ua]], writes=[gTb[fi]])

                    for i in range(-2, NCH + 6):
                        if 0 <= i - 5 and (i - 5) % 4 == 0 and (i - 5) // 4 < 11:
                            sGate((i - 5) // 4)
                        if 0 <= i + 2 < NCH:
                            sPE(i + 2)
                        if 0 <= i + 1 < NCH:
                            sIn(i + 1)
                        if 0 <= i < NCH:
                            sMid(i)
                        if 0 <= i - 1 < NCH:
                            sSTT(i - 1)
                    xt = xts[xi]
                    for j in range(8):
                        wt, wtb = next_w(("fo", l, j))
                        wv = v3(wt, FC, 128)
                        pi = ps_ring2.next()
                        for fc in range(FC):
                            P.op("pe", lambda e, fc=fc, pi=pi, wv=wv: e.matmul(
                                psums[pi][:], wv[:, fc, :], gT[:, fc, :], start=(fc == 0), stop=(fc == FC - 1)),
                                reads=[wtb, gTb[fc]], writes=[psb[pi]])
                        P.op("dve", lambda e, j=j, pi=pi, xt=xt: e.tensor_tensor(
                            out=xt[:, j, :], in0=psums[pi][:], in1=xt[:, j, :], op=ALU.add),
                            reads=[psb[pi], xtb[xi]], writes=[xtb[xi]])
                    store_x(dst_v, it, xi, dstbufs[it])
                P.barrier()

        def sb_phase(j, l, src_v, srcbufs, dst_v, dstbufs):
            NB = S // 128
            NG = S // 512
            oscr = nc.dram_tensor(f"oscr{j}", [SB_H, 64, S], BF16, kind="Internal").ap()
            oscr_v = oscr.rearrange("h p s -> p h s")
            oscrb = [Buf() for _ in range(NG)]
            amask = k16("amask")
            ph = ExitStack()
            with ph:
                def al(name, shape, dt):
                    return ph.enter_context(nc.sbuf_tensor(f"{name}_{j}", shape, dt))
                hTall = al("hTall", [128, DC, S], BF16)
                hTallb = [Buf() for _ in range(NT)]
                cur = load_x(src_v, 0, srcbufs[0])
                for it in range(NT):
                    xi = cur
                    if it + 1 < NT:
                        cur = load_x(src_v, it + 1, srcbufs[it + 1])
                    rms_stats(xi)
                    rms_apply(xi, "mixg", l * 8, hTall[:, :, it * TT:(it + 1) * TT], hTallb[it])
                QT = al("QT", [128, S], BF16)
                KT = al("KT", [128, S], BF16)
                V = al("V", [128, NB, 128], BF16)
                QTb = [Buf() for _ in range(NT)]
                KTb = [Buf() for _ in range(NT)]
                Vb = [Buf() for _ in range(NT)]
                e_ = [[al(f"e{c}{q}", [128, 512], F32) for q in range(2)] for c in range(2)]
                spb_ = [[al(f"spb{c}{q}", [128, 512], BF16) for q in range(2)] for c in range(2)]
                ex_ = [[al(f"ex{c}{q}", [128, 512], BF16) for q in range(2)] for c in range(2)]
                Wb_ = [[al(f"Wb{c}{q}", [128, 512], BF16) for q in range(2)] for c in range(2)]
                osb_ = [al(f"osb{c}", [64, 512], BF16) for c in range(2)]
                eB = [[Buf() for q in range(2)] for c in range(2)]
                spbB = [[Buf() for q in range(2)] for c in range(2)]
                exB = [[Buf() for q in range(2)] for c in range(2)]
                WbB = [[Buf() for q in range(2)] for c in range(2)]
                osbB = [Buf() for _ in range(2)]
                SBK = [0, 1]
                CBK = [2, 3]
                OBK = [[4, 5], [6, 7]]
                ps_ring_sb = Ring([0, 1])
                for hp in range(8):
                    wt, wtb = next_w(("sq", j, hp))
                    wv = v3(wt, 8, 384)
                    for it in range(NT):
                        tsl = slice(it * TT, (it + 1) * TT)
                        pi = ps_ring_sb.next()
                        for dc in range(DC):
                            P.op("pe", lambda e, dc=dc, pi=pi, wv=wv, tsl=tsl: e.matmul(
                                psums[pi][:], wv[:, dc, 0:128], hTall[:, dc, tsl], start=(dc == 0), stop=(dc == DC - 1)),
                                reads=[wtb, hTallb[it]], writes=[psb[pi]])
                        P.op("act", lambda e, pi=pi, tsl=tsl: e.activation(out=QT[:, tsl], in_=psums[pi][:], func=AF.Copy),
                             reads=[psb[pi]], writes=[QTb[it]])
                        pi = ps_ring_sb.next()
                        for dc in range(DC):
                            P.op("pe", lambda e, dc=dc, pi=pi, wv=wv, tsl=tsl: e.matmul(
                                psums[pi][:], wv[:, dc, 128:256], hTall[:, dc, tsl], start=(dc == 0), stop=(dc == DC - 1)),
                                reads=[wtb, hTallb[it]], writes=[psb[pi]])
                        P.op("dve", lambda e, pi=pi, tsl=tsl: e.tensor_copy(out=KT[:, tsl], in_=psums[pi][:]),
                             reads=[psb[pi]], writes=[KTb[it]])
                        pi = ps_ring_sb.next()
                        for blk in range(4):
                            t0 = it * TT + blk * 128
                            for dc in range(DC):
                                P.op("pe", lambda e, dc=dc, pi=pi, wv=wv, t0=t0, blk=blk: e.matmul(
                                    psums[pi][:, blk * 128:(blk + 1) * 128], hTall[:, dc, t0:t0 + 128], wv[:, dc, 256:384],
                                    start=(dc == 0), stop=(dc == DC - 1)),
                                    reads=[wtb, hTallb[it]], writes=[psb[pi]])
                        P.op("act", lambda e, pi=pi, it=it: e.activation(
                            out=V[:, 4 * it:4 * it + 4, :], in_=psums[pi][:].rearrange("p (b n) -> p b n", b=4), func=AF.Copy),
                            reads=[psb[pi]], writes=[Vb[it]])
                    steps = [(g, jb) for g in range(NG) for jb in range(4 * g + 3, -1, -1)]
                    nst = len(steps)

                    def st_info(i):
                        g, jb = steps[i]
                        return g, jb, jb == 4 * g + 3, jb == 0, jb - 4 * g

                    def stA(i):
                        g, jb, first, last, r = st_info(i)
                        ksl = slice(jb * 128, (jb + 1) * 128)
                        qsl = slice(g * 512, (g + 1) * 512)
                        for c in range(2):
                            lo = 64 * c
                            P.op("pe", lambda e, c=c, lo=lo, ksl=ksl, qsl=qsl, r=r: e.matmul(
                                psums[SBK[c]][:], KT[lo:lo + 64, ksl], QT[lo:lo + 64, qsl], start=True, stop=(r < 0)),
                                reads=[KTb[jb // 4], QTb[g]], writes=[psb[SBK[c]]])
                            if r >= 0:
                                P.op("pe", lambda e, c=c, r=r: e.matmul(
                                    psums[SBK[c]][:], k16("identb"), amask[:, r * 512:(r + 1) * 512], start=False, stop=True),
                                    reads=[cbuf], writes=[psb[SBK[c]]])

                    def stBC(i):
                        q = i % 2
                        for c in range(2):
                            P.op("act", lambda e, c=c, q=q: e.activation(
                                out=e_[c][q][:], in_=psums[SBK[c]][:], func=AF.Exp, scale=SB_DH ** -0.5),
                                reads=[psb[SBK[c]]], writes=[eB[c][q]])
                        for c in range(2):
                            P.op("act", lambda e, c=c, q=q: e.activation(
                                out=spb_[c][q][:], in_=e_[c][q][:], func=AF.Ln, bias=sm("one")),
                                reads=[eB[c][q], cbuf], writes=[spbB[c][q]])

                    def stD(i):
                        g, jb, first, last, r = st_info(i)
                        q = i % 2
                        for c in range(2):
                            P.op("pe", lambda e, c=c, q=q, first=first: e.matmul(
                                psums[CBK[c]][:], k16("negUi"), spb_[c][q][:], start=first, stop=True, skip_group_check=True),
                                reads=[spbB[c][q], cbuf], writes=[psb[CBK[c]]])

                    def stE(i):
                        q = i % 2
                        for c in range(2):
                            P.op("act", lambda e, c=c, q=q: e.activation(out=ex_[c][q][:], in_=psums[CBK[c]][:], func=AF.Exp),
                                 reads=[psb[CBK[c]]], writes=[exB[c][q]])

                    def stF(i):
                        g, jb, first, last, r = st_info(i)
                        q = i % 2
                        if last:
                            return
                        for c in range(2):
                            P.op("pe", lambda e, c=c, q=q: e.matmul(
                                psums[CBK[c]][:], k16("negLs"), spb_[c][q][:], start=False, stop=True, skip_group_check=True),
                                reads=[spbB[c][q], cbuf], writes=[psb[CBK[c]]])

                    def stG(i):
                        q = i % 2
                        for c in range(2):
                            P.op("dve" if c == 0 else "pool", lambda e, c=c, q=q: e.tensor_tensor(
                                out=Wb_[c][q][:], in0=e_[c][q][:], in1=ex_[c][q][:], op=ALU.mult),
                                reads=[eB[c][q], exB[c][q]], writes=[WbB[c][q]])

                    def stH(i):
                        g, jb, first, last, r = st_info(i)
                        q = i % 2
                        ob = g % 2
                        for c in range(2):
                            P.op("pe", lambda e, c=c, q=q, jb=jb, first=first, last=last, ob=ob: e.matmul(
                                psums[OBK[c][ob]][0:64, :], V[:, jb, 64 * c:64 * c + 64], Wb_[c][q][:], start=first, stop=last),
                                reads=[Vb[jb // 4], WbB[c][q]], writes=[psb[OBK[c][ob]]])
                        if last:
                            qsl = slice(g * 512, (g + 1) * 512)
                            for c in range(2):
                                P.op("dve" if c == 1 else "act", (lambda e, c=c, ob=ob: e.tensor_copy(
                                    out=osb_[c][:], in_=psums[OBK[c][ob]][0:64, :])) if c == 1 else (lambda e, c=c, ob=ob: e.activation(
                                    out=osb_[c][:], in_=psums[OBK[c][ob]][0:64, :], func=AF.Copy)),
                                    reads=[psb[OBK[c][ob]]], writes=[osbB[c]])
                                P.dma("sp", lambda e, c=c, hp=hp, qsl=qsl: e.dma_start(out=oscr[2 * hp + c][:, qsl], in_=osb_[c][:]),
                                      f"ov{c}", reads=[osbB[c]], writes=[oscrb[g]])

                    for i in range(-1, nst + 1):
                        if i + 1 < nst:
                            stA(i + 1)
                        if 0 <= i - 1 < nst:
                            stF(i - 1)
                            stH(i - 1)
                        if 0 <= i < nst:
                            stD(i)
                        if i + 1 < nst:
                            stBC(i + 1)
                        if 0 <= i < nst:
                            stE(i)
                            stG(i)
                P.barrier()
            ph = ExitStack()
            with ph:
                ots = [ph.enter_context(nc.sbuf_tensor(f"otile{j}_{i}", [64, SB_H, TT], BF16)) for i in range(2)]
                otB = [Buf() for _ in range(2)]
                cur = load_x(src_v, 0, srcbufs[0])
                for it in range(NT):
                    xi = cur
                    if it + 1 < NT:
                        cur = load_x(src_v, it + 1, srcbufs[it + 1])
                    oi = it % 2
                    P.dma("sp", lambda e, oi=oi, it=it: e.dma_start(out=ots[oi][:], in_=oscr_v[:, :, it * TT:(it + 1) * TT]),
                          f"ol{oi}", reads=[oscrb[it]], writes=[otB[oi]])
                    xt = xts[xi]
                    for jn in range(8):
                        wt, wtb = next_w(("so", j, jn))
                        wv = wt[0:64, 0:2048].rearrange("p (h n) -> p h n", h=SB_H)
                        pi = ps_ring2.next()
                        for h in range(SB_H):
                            P.op("pe", lambda e, h=h, pi=pi, wv=wv, oi=oi: e.matmul(
                                psums[pi][:], wv[:, h, :], ots[oi][:, h, :], start=(h == 0), stop=(h == SB_H - 1)),
                                reads=[wtb, otB[oi]], writes=[psb[pi]])
                        P.op("dve", lambda e, jn=jn, pi=pi, xt=xt: e.tensor_tensor(
                            out=xt[:, jn, :], in0=psums[pi][:], in1=xt[:, jn, :], op=ALU.add),
                            reads=[psb[pi], xtb[xi]], writes=[xtb[xi]])
                    store_x(dst_v, it, xi, dstbufs[it])
                P.barrier()

        def ssd_phase(j, l, src_v, srcbufs, dst_v, dstbufs):
            ph = ExitStack()
            with ph:
                def al(name, shape, dt):
                    return ph.enter_context(nc.sbuf_tensor(f"{name}_s{j}", shape, dt))
                hT = al("hT", [128, DC, TT], BF16)
                hTb = Buf()
                state = al("state", [128, 2048], F32)
                stateb = Buf()
                sbf = al("sbf", [128, 2048], BF16)
                sbfb = Buf()
                tails = al("tails", [128, 32, 3], F32)
                tailb = Buf()
                negA = al("negA", [128, 32], F32)
                negAb = Buf()
                ubs = [al(f"ub{i}", [128, TT + 3], F32) for i in range(2)]
                ubb = [Buf() for _ in range(2)]
                accs = [al(f"acc{i}", [128, TT], F32) for i in range(2)]
                accb = [Buf() for _ in range(2)]
                cs = al("cs", [128, TT], F32)
                csb = Buf()
                xs_tok = al("xs_tok", [128, 4, 2048], BF16)
                xs_tokb = [Buf() for _ in range(16)]
                B_tok = al("B_tok", [128, 4, 1024], BF16)
                B_tokb = [Buf() for _ in range(8)]
                BT = al("BT", [128, 8, TT], BF16)
                BTb = [Buf() for _ in range(8)]
                CT = al("CT", [128, 8, TT], BF16)
                CTb = [Buf() for _ in range(8)]
                sz = al("sz", [128, 4, 2048], BF16)
                szb = [Buf() for _ in range(4)]
                dtx = al("dtx", [128, 128], F32)
                dtxb = Buf()
                dtt = al("dtt", [128, 128], F32)
                dttb = Buf()
                at = al("at", [128, 128], F32)
                atb = Buf()
                acs = al("acs", [128, 64], F32)
                acsb = Buf()
                eall = al("eall", [128, 64], F32)
                eallb = Buf()
                dlt = al("dlt", [128, 32], F32)
                dltb = Buf()
                wsc = al("wsc", [128, 32], F32)
                wscb = Buf()
                decs = [al(f"dec{i}", [128, 512], F32) for i in range(2)]
                decb = [Buf() for _ in range(2)]
                cbms = [al(f"cbm{i}", [128, 128], F32) for i in range(2)]
                cbmb = [Buf() for _ in range(2)]
                scs = [al(f"sc{i}", [128, 512], BF16) for i in range(2)]
                scb_ = [Buf() for _ in range(2)]
                xdt = al("xdt", [128, 2048], BF16)
                xdtb = Buf()
                xw = al("xw", [128, 2048], BF16)
                xwb = Buf()
                xsD = al("xsD", [128, 2048], BF16)
                xsDb = Buf()
                yscs = [al(f"ysc{i}", [128, 256], F32) for i in range(2)]
                yscb = [Buf() for _ in range(2)]
                y = al("y", [128, 2048], F32)
                yb = Buf()
                ysq = al("ysq", [128, 2048], BF16)
                ysqb = Buf()
                ss = al("ss", [128, 1], F32)
                lnss = al("lnss", [128, 1], F32)
                rs = al("rs", [128, 1], F32)
                ssb, lnssb, rsb = Buf(), Buf(), Buf()
                yTc = al("yTc", [128, 16, 128], BF16)
                yTcb = Buf()
                LA = xts[1][:].rearrange("p a (b c) -> p (a b) c", c=128)
                LAb = xtb[1]
                ident32 = k32("ident32")
                triR = k32("triR")
                rr = Ring([0, 1])
                rr2 = Ring([0, 1])
                rr3 = Ring([0, 1])

                P.op("pool", lambda e: e.memset(tails[:], 0.0), writes=[tailb])
                P.op("pool", lambda e: e.memset(state[:], 0.0), writes=[stateb])
                P.op("pool", lambda e: e.memset(sbf[:], 0.0), writes=[sbfb])
                P.op("act", lambda e: e.activation(out=negA[:], in_=sm("alog", j * 32, 32), func=AF.Exp),
                     reads=[cbuf], writes=[negAb])
                P.op("dve", lambda e: e.tensor_scalar(out=negA[:], in0=negA[:], scalar1=-1.0, scalar2=None, op0=ALU.mult),
                     reads=[negAb], writes=[negAb])

                for it in range(NT):
                    xi = load_x(src_v, it, srcbufs[it], slot=0)
                    xt = xts[xi]
                    rms_stats(xi)
                    rms_apply(xi, "mixg", l * 8, hT, hTb)
                    for xg in range(8):
                        wt, wtb = next_w(("sx", j, xg))
                        wv = v3(wt, 8, 512)
                        for c in range(4):
                            ch = 4 * xg + c
                            pi = ps_ring.next()
                            for dc in range(DC):
                                P.op("pe", lambda e, dc=dc, c=c, pi=pi, wv=wv: e.matmul(
                                    psums[pi][:], wv[:, dc, c * 128:(c + 1) * 128], hT[:, dc, :],
                                    start=(dc == 0), stop=(dc == DC - 1)),
                                    reads=[wtb, hTb], writes=[psb[pi]])
                            ui = rr.next()
                            ub = ubs[ui]
                            acc = accs[ui]
                            P.op("dve", lambda e, ub=ub, ch=ch: e.tensor_copy(out=ub[:, 0:3], in_=tails[:, ch, :]),
                                 reads=[tailb], writes=[ubb[ui]])
                            P.op("act", lambda e, ub=ub, pi=pi: e.activation(out=ub[:, 3:TT + 3], in_=psums[pi][:], func=AF.Copy),
                                 reads=[psb[pi]], writes=[ubb[ui]])
                            P.op("dve", lambda e, ub=ub, ch=ch: e.tensor_copy(out=tails[:, ch, :], in_=ub[:, TT:TT + 3]),
                                 reads=[ubb[ui]], writes=[tailb])
                            wk = [sm("scw", (j * 4 + k) * 32 + ch) for k in range(4)]
                            bb = sm("scb", j * 32 + ch)
                            P.op("pool", lambda e, ub=ub, acc=acc, wk=wk, bb=bb: e.tensor_scalar(
                                out=acc[:], in0=ub[:, 3:TT + 3], scalar1=wk[3], scalar2=bb, op0=ALU.mult, op1=ALU.add),
                                reads=[ubb[ui], cbuf], writes=[accb[ui]])
                            for k in range(3):
                                P.op("dve", lambda e, ub=ub, acc=acc, wk=wk, k=k: e.scalar_tensor_tensor(
                                    out=acc[:], in0=ub[:, k:TT + k], scalar=wk[k], in1=acc[:], op0=ALU.mult, op1=ALU.add),
                                    reads=[ubb[ui], accb[ui], cbuf], writes=[accb[ui]])
                            if ch < 16 or ch < 24:
                                P.op("act", lambda e, acc=acc: e.activation(out=cs[:], in_=acc[:], func=AF.Silu),
                                     reads=[accb[ui]], writes=[csb])
                                if ch >= 16:
                                    g = ch - 16
                                    P.op("pool", lambda e, g=g: e.tensor_copy(out=BT[:, g, :], in_=cs[:]),
                                         reads=[csb], writes=[BTb[g]])
                                pt = ps_ring.next()
                                for blk in range(4):
                                    P.op("pe", lambda e, blk=blk, pt=pt: e.transpose(
                                        psums[pt][:, blk * 128:(blk + 1) * 128], cs[:, blk * 128:(blk + 1) * 128], ident32),
                                        reads=[csb, cbuf], writes=[psb[pt]])
                                pv = psums[pt][:].rearrange("p (b n) -> p b n", b=4)
                                if ch < 16:
                                    P.op("dve", lambda e, pv=pv, ch=ch: e.tensor_copy(
                                        out=xs_tok[:, :, ch * 128:(ch + 1) * 128], in_=pv),
                                        reads=[psb[pt]], writes=[xs_tokb[ch]])
                                else:
                                    g = ch - 16
                                    P.op("dve", lambda e, pv=pv, g=g: e.tensor_copy(
                                        out=B_tok[:, :, g * 128:(g + 1) * 128], in_=pv),
                                        reads=[psb[pt]], writes=[B_tokb[g]])
                            else:
                                g = ch - 24
                                P.op("act", lambda e, acc=acc, g=g: e.activation(out=CT[:, g, :], in_=acc[:], func=AF.Silu),
                                     reads=[accb[ui]], writes=[CTb[g]])
                    for zg in range(4):
                        wt, wtb = next_w(("sz", j, zg))
                        wv = v3(wt, 8, 512)
                        for blk in range(4):
                            pi = ps_ring.next()
                            for dc in range(DC):
                                P.op("pe", lambda e, dc=dc, blk=blk, pi=pi, wv=wv: e.matmul(
                                    psums[pi][:], hT[:, dc, blk * 128:(blk + 1) * 128], wv[:, dc, :],
                                    start=(dc == 0), stop=(dc == DC - 1)),
                                    reads=[wtb, hTb], writes=[psb[pi]])
                            P.op("act", lambda e, blk=blk, zg=zg, pi=pi: e.activation(
                                out=sz[:, blk, zg * 512:(zg + 1) * 512], in_=psums[pi][:], func=AF.Silu),
                                reads=[psb[pi]], writes=[szb[blk]])
                    wt, wtb = next_w(("sdt", j))
                    wv = v3(wt, 8, 32)
                    pi = ps_ring.next()
                    for blk in range(4):
                        for dc in range(DC):
                            P.op("pe", lambda e, dc=dc, blk=blk, pi=pi, wv=wv: e.matmul(
                                psums[pi][:, blk * 32:(blk + 1) * 32], hT[:, dc, blk * 128:(blk + 1) * 128], wv[:, dc, :],
                                start=(dc == 0), stop=(dc == DC - 1)),
                                reads=[wtb, hTb], writes=[psb[pi]])
                    dtb_bc = sm("dtb", j * 32, 32).unsqueeze(1).to_broadcast([128, 4, 32])
                    P.op("dve", lambda e, pi=pi: e.tensor_tensor(
                        out=dtx[:].rearrange("p (b h) -> p b h", b=4), in0=psums[pi][:, 0:128].rearrange("p (b h) -> p b h", b=4),
                        in1=dtb_bc, op=ALU.add),
                        reads=[psb[pi], cbuf], writes=[dtxb])
                    P.op("act", lambda e: e.activation(out=dtx[:], in_=dtx[:], func=AF.Exp), reads=[dtxb], writes=[dtxb])
                    P.op("act", lambda e: e.activation(out=dtt[:], in_=dtx[:], func=AF.Ln, bias=sm("one")),
                         reads=[dtxb, cbuf], writes=[dttb])
                    P.op("dve", lambda e: e.tensor_tensor(
                        out=at[:].rearrange("p (b h) -> p b h", b=4), in0=dtt[:].rearrange("p (b h) -> p b h", b=4),
                        in1=negA[:].unsqueeze(1).to_broadcast([128, 4, 32]), op=ALU.mult),
                        reads=[dttb, negAb], writes=[atb])
                    for blk in range(4):
                        csl = slice(blk * 128, (blk + 1) * 128)
                        a_blk = at[:, blk * 32:(blk + 1) * 32]
                        dt_blk = dtt[:, blk * 32:(blk + 1) * 32]
                        pst = ps_ring.next()
                        P.op("pe", lambda e, pst=pst, a_blk=a_blk: e.matmul(psums[pst][:, 0:32], triR, a_blk, start=True, stop=True),
                             reads=[atb, cbuf], writes=[psb[pst]])
                        P.op("pe", lambda e, pst=pst, a_blk=a_blk: e.matmul(psums[pst][:, 32:64], k32("ones32"), a_blk, start=True, stop=True),
                             reads=[atb, cbuf], writes=[psb[pst]])
                        P.op("act", lambda e, pst=pst: e.activation(out=acs[:], in_=psums[pst][:, 0:64], func=AF.Copy),
                             reads=[psb[pst]], writes=[acsb])
                        P.op("act", lambda e: e.activation(out=eall[:], in_=acs[:], func=AF.Exp), reads=[acsb], writes=[eallb])
                        P.op("dve", lambda e: e.tensor_tensor(out=dlt[:], in0=acs[:, 32:64], in1=acs[:, 0:32], op=ALU.subtract),
                             reads=[acsb], writes=[dltb])
                        P.op("act", lambda e: e.activation(out=wsc[:], in_=dlt[:], func=AF.Exp), reads=[dltb], writes=[wscb])
                        P.op("pool", lambda e, a_blk=a_blk: e.tensor_tensor(
                            out=LA, in0=k32("ustr").unsqueeze(1).to_broadcast([128, 32, 128]),
                            in1=a_blk.unsqueeze(2).to_broadcast([128, 32, 128]), op=ALU.mult),
                            reads=[atb, cbuf], writes=[LAb])
                        xs_v = xs_tok[:, blk, :].rearrange("p (h d) -> p h d", h=32)
                        P.op("pool", lambda e, xs_v=xs_v, dt_blk=dt_blk: e.tensor_tensor(
                            out=xdt[:].rearrange("p (h d) -> p h d", h=32), in0=xs_v,
                            in1=dt_blk.unsqueeze(2).to_broadcast([128, 32, 64]), op=ALU.mult),
                            reads=xs_tokb + [dttb], writes=[xdtb])
                        P.op("dve", lambda e: e.tensor_tensor(
                            out=xw[:].rearrange("p (h d) -> p h d", h=32), in0=xdt[:].rearrange("p (h d) -> p h d", h=32),
                            in1=wsc[:].unsqueeze(2).to_broadcast([128, 32, 64]), op=ALU.mult),
                            reads=[xdtb, wscb], writes=[xwb])
                        P.op("pool", lambda e, xs_v=xs_v: e.tensor_tensor(
                            out=xsD[:].rearrange("p (h d) -> p h d", h=32), in0=xs_v,
                            in1=sm("dsk", j * 32, 32).unsqueeze(2).to_broadcast([128, 32, 64]), op=ALU.mult),
                            reads=xs_tokb + [cbuf], writes=[xsDb])
                        for g in range(8):
                            pg = ps_ring.next()
                            for hh in range(4):
                                P.op("pe", lambda e, pg=pg, hh=hh, g=g: e.matmul(
                                    psums[pg][:, hh * 128:(hh + 1) * 128], LA[:, 4 * g + hh, :], triR, start=True, stop=True),
                                    reads=[LAb, cbuf], writes=[psb[pg]])
                            di = rr2.next()
                            P.op("act", lambda e, pg=pg, di=di: e.activation(out=decs[di][:], in_=psums[pg][:], func=AF.Exp),
                                 reads=[psb[pg]], writes=[decb[di]])
                            pc = ps_ring.next()
                            P.op("pe", lambda e, pc=pc, g=g, csl=csl: e.matmul(
                                psums[pc][:, 0:128], BT[:, g, csl], CT[:, g, csl], start=True, stop=True),
                                reads=[BTb[g], CTb[g]], writes=[psb[pc]])
                            P.op("dve", lambda e, pc=pc, di=di: e.tensor_tensor(
                                out=cbms[di][:], in0=psums[pc][:, 0:128], in1=k32("cmask"), op=ALU.mult),
                                reads=[psb[pc], cbuf], writes=[cbmb[di]])
                            P.op("pool", lambda e, di=di: e.tensor_tensor(
                                out=scs[di][:].rearrange("p (h t) -> p h t", h=4), in0=decs[di][:].rearrange("p (h t) -> p h t", h=4),
                                in1=cbms[di][:].unsqueeze(1).to_broadcast([128, 4, 128]), op=ALU.mult),
                                reads=[decb[di], cbmb[di]], writes=[scb_[di]])
                            py = ps_ring2.next()
                            for hh in range(4):
                                h = 4 * g + hh
                                P.op("pe", lambda e, py=py, hh=hh, h=h, di=di: e.matmul(
                                    psums[py][:, hh * 64:(hh + 1) * 64], scs[di][:, hh * 128:(hh + 1) * 128],
                                    xdt[:, h * 64:(h + 1) * 64], start=(hh == 0), stop=False, skip_group_check=True),
                                    reads=[scb_[di], xdtb], writes=[psb[py]])
                            P.op("pe", lambda e, py=py, g=g: e.matmul(
                                psums[py][:, 0:256], k16("identb"), xsD[:, g * 256:(g + 1) * 256], start=False, stop=True,
                                skip_group_check=True),
                                reads=[xsDb, cbuf], writes=[psb[py]])
                            P.op("pe", lambda e, py=py, g=g, csl=csl: e.matmul(
                                psums[py][:, 256:512], CT[:, g, csl], sbf[:, g * 256:(g + 1) * 256], start=False, stop=True,
                                skip_group_check=True),
                                reads=[CTb[g], sbfb], writes=[psb[py]])
                            yi = rr3.next()
                            P.op("dve", lambda e, py=py, g=g, yi=yi: e.tensor_tensor(
                                out=yscs[yi][:].rearrange("p (h d) -> p h d", h=4),
                                in0=psums[py][:, 256:512].rearrange("p (h d) -> p h d", h=4),
                                in1=eall[:, 4 * g:4 * g + 4].unsqueeze(2).to_broadcast([128, 4, 64]), op=ALU.mult),
                                reads=[psb[py], eallb], writes=[yscb[yi]])
                            P.op("dve", lambda e, py=py, g=g, yi=yi: e.tensor_tensor(
                                out=y[:, g * 256:(g + 1) * 256], in0=psums[py][:, 0:256], in1=yscs[yi][:], op=ALU.add),
                                reads=[psb[py], yscb[yi]], writes=[yb])
                        P.op("pool", lambda e: e.tensor_tensor(
                            out=state[:].rearrange("p (h d) -> p h d", h=32), in0=state[:].rearrange("p (h d) -> p h d", h=32),
                            in1=eall[:, 32:64].unsqueeze(2).to_broadcast([128, 32, 64]), op=ALU.mult),
                            reads=[stateb, eallb], writes=[stateb])
                        for g2 in range(4):
                            pu = ps_ring.next()
                            for q in range(2):
                                g = 2 * g2 + q
                                P.op("pe", lambda e, pu=pu, q=q, g=g, blk=blk: e.matmul(
                                    psums[pu][:, q * 256:(q + 1) * 256], B_tok[:, blk, g * 128:(g + 1) * 128],
                                    xw[:, g * 256:(g + 1) * 256], start=True, stop=True),
                                    reads=[B_tokb[g], xwb], writes=[psb[pu]])
                            P.op("dve", lambda e, pu=pu, g2=g2: e.tensor_tensor(
                                out=state[:, g2 * 512:(g2 + 1) * 512], in0=psums[pu][:], in1=state[:, g2 * 512:(g2 + 1) * 512], op=ALU.add),
                                reads=[psb[pu], stateb], writes=[stateb])
                        P.op("pool", lambda e: e.tensor_copy(out=sbf[:], in_=state[:]), reads=[stateb], writes=[sbfb])
                        P.op("pool", lambda e, blk=blk: e.tensor_tensor(out=y[:], in0=y[:], in1=sz[:, blk, :], op=ALU.mult),
                             reads=[yb, szb[blk]], writes=[yb])
                        P.op("act", lambda e: e.activation(out=ysq[:], in_=y[:], func=AF.Square, accum_out=ss[:]),
                             reads=[yb], writes=[ysqb, ssb])
                        P.op("act", lambda e: e.activation(out=lnss[:], in_=ss[:], func=AF.Ln, scale=1.0 / SSD_DI, bias=sm("eps")),
                             reads=[ssb, cbuf], writes=[lnssb])
                        P.op("act", lambda e: e.activation(out=rs[:], in_=lnss[:], func=AF.Exp, scale=-0.5),
                             reads=[lnssb], writes=[rsb])
                        P.op("dve", lambda e: e.tensor_scalar(out=y[:], in0=y[:], scalar1=rs[:, 0:1], scalar2=None, op0=ALU.mult),
                             reads=[yb, rsb], writes=[yb])
                        for c4 in range(4):
                            pt = ps_ring.next()
                            for q in range(4):
                                cc = 4 * c4 + q
                                P.op("pe", lambda e, pt=pt, q=q, cc=cc: e.transpose(
                                    psums[pt][:, q * 128:(q + 1) * 128], y[:, cc * 128:(cc + 1) * 128], ident32),
                                    reads=[yb, cbuf], writes=[psb[pt]])
                            for q in range(4):
                                cc = 4 * c4 + q
                                P.op("act" if q % 2 == 0 else "dve", (lambda e, pt=pt, q=q, cc=cc: e.activation(
                                    out=yTc[:, cc, :], in_=psums[pt][:, q * 128:(q + 1) * 128], func=AF.Identity,
                                    scale=sm("sng", j * 16 + cc))) if q % 2 == 0 else (lambda e, pt=pt, q=q, cc=cc: e.tensor_scalar(
                                    out=yTc[:, cc, :], in0=psums[pt][:, q * 128:(q + 1) * 128], scalar1=sm("sng", j * 16 + cc),
                                    scalar2=None, op0=ALU.mult)),
                                    reads=[psb[pt], cbuf], writes=[yTcb])
                        for k in range(4):
                            wt, wtb = next_w(("sout", j, k))
                            wv = v3(wt, 16, 256)
                            for nl in range(2):
                                pi = ps_ring2.next()
                                for cc in range(16):
                                    P.op("pe", lambda e, pi=pi, cc=cc, nl=nl, wv=wv: e.matmul(
                                        psums[pi][:, 0:128], wv[:, cc, nl * 128:(nl + 1) * 128], yTc[:, cc, :],
                                        start=(cc == 0), stop=(cc == 15)),
                                        reads=[wtb, yTcb], writes=[psb[pi]])
                                P.op("dve", lambda e, pi=pi, k=k, nl=nl, csl=csl, xt=xt: e.tensor_tensor(
                                    out=xt[:, 2 * k + nl, csl], in0=psums[pi][:, 0:128], in1=xt[:, 2 * k + nl, csl], op=ALU.add),
                                    reads=[psb[pi], xtb[xi]], writes=[xtb[xi]])
                    store_x(dst_v, it, xi, dstbufs[it])
                P.barrier()

        def final_phase(src_v, srcbufs):
            ph = ExitStack()
            with ph:
                ots = [ph.enter_context(nc.sbuf_tensor(f"ot{i}", [128, DC, TT], F32)) for i in range(2)]
                otb = [Buf() for _ in range(2)]
                cur = load_x(src_v, 0, srcbufs[0])
                for it in range(NT):
                    xi = cur
                    if it + 1 < NT:
                        cur = load_x(src_v, it + 1, srcbufs[it + 1])
                    rms_stats(xi)
                    o = it % 2
                    rms_apply(xi, "fing", 0, ots[o], otb[o])
                    P.dma("pool", lambda e, o=o, it=it: e.dma_start(out=out_v[:, :, it * TT:(it + 1) * TT], in_=ots[o][:]),
                          f"os{o}", reads=[otb[o]], writes=[outb])
                P.barrier()

        outb = Buf("out")
        src_v, srcb = (xin_v, [None] * NT)
        for item in plan:
            if item[0] == "ffn":
                ffn_phase(item[1], src_v, srcb, xres_v, xres_b)
            if item[0] == "sb":
                sb_phase(item[1], item[2], src_v, srcb, xres_v, xres_b)
            if item[0] == "ssd":
                ssd_phase(item[1], item[2], src_v, srcb, xres_v, xres_b)
            src_v, srcb = xres_v, xres_b
        if do_final:
            final_phase(src_v, srcb)
        else:
            cur = None
            for it in range(NT):
                xi = load_x(src_v, it, srcb[it])
                P.dma("pool", lambda e, xi=xi, it=it: e.dma_start(out=out_v[:, :, it * TT:(it + 1) * TT], in_=xts[xi][:]),
                      f"os{xi}", reads=[xtb[xi]], writes=[outb])
            P.barrier()
        P.barrier()
        P.emit()
    return nc


def prep_inputs(inp):
    small, scols = pack_small(inp)
    eps = np.full((128, 1), NORM_EPS, np.float32)
    scols["eps"] = (small.shape[1], 1)
    scols["one"] = (small.shape[1] + 1, 1)
    small = np.concatenate([small, eps, np.ones((128, 1), np.float32)], axis=1)
    c16, c16cols, c32, c32cols = make_consts()
    return small, scols, c16, c16cols, c32, c32cols


FULL_PLAN = [("ssd", 0, 0), ("ffn", 0), ("sb", 0, 1), ("ffn", 1), ("ssd", 1, 2), ("ffn", 2), ("sb", 1, 3), ("ffn", 3)]


def run_plan(inp, plan, S, x_batch, do_final=True, ncores=8, trace=False):
    small, scols, c16, c16cols, c32, c32cols = prep_inputs(inp)
    nc = build_program(S, plan, scols, small.shape[1], c16cols, c16.shape[1], c32cols, c32.shape[1], do_final=do_final)
    kinds = set(p[0] for p in plan)
    shared = {"small": small, "c16": c16, "c32": c32}
    if "ffn" in kinds:
        shared["ffn_w_in"] = np.ascontiguousarray(inp["ffn_w_in"], np.float32)
        shared["ffn_w_out"] = np.ascontiguousarray(inp["ffn_w_out"], np.float32)
    if "ssd" in kinds:
        shared["ssd_w_in"] = np.ascontiguousarray(inp["ssd_w_in"], np.float32)
        shared["ssd_w_out"] = np.ascontiguousarray(inp["ssd_w_out"], np.float32)
    if "sb" in kinds:
        shared["sb_w_qkv"] = np.ascontiguousarray(inp["sb_w_qkv"], np.float32)
        shared["sb_w_out"] = np.ascontiguousarray(inp["sb_w_out"], np.float32)
    in_maps = []
    for b in range(ncores):
        m = dict(shared)
        m["xT"] = np.ascontiguousarray(x_batch[b].T)
        in_maps.append(m)
    res = run_bass_kernel_spmd(nc, in_maps, core_ids=list(range(ncores)), trace=trace)
    out = np.stack([np.ascontiguousarray(r["outT"].T) for r in res.results], axis=0)
    return out, res


def kernel(**inputs):
    x = np.asarray(inputs["x"], np.float32)
    out, _ = run_plan(inputs, FULL_PLAN, x.shape[1], x)
    return out.astype(np.float32)
```

```python
from contextlib import ExitStack
import numpy as np
import concourse.bass as bass
import concourse.mybir as mybir
from concourse.bass_utils import run_bass_kernel_spmd

F32 = mybir.dt.float32
BF16 = mybir.dt.bfloat16
AF = mybir.ActivationFunctionType
ALU = mybir.AluOpType

D = 1024
DC = 8
NORM_EPS = 1e-6
DFF = 2816
FC = DFF // 128
TT = 512
WSLOT = 4096
NWS = 3

SSD_DI = 2048
SSD_H = 32
SSD_P = 64
SSD_G = 8
SSD_N = 128
SSD_CONVD = 4096
SSD_IN = 6176
SB_H = 16
SB_DH = 64
NEG = -30000.0


class Buf:
    __slots__ = ("w", "r", "name")

    def __init__(self, name=""):
        self.w = None
        self.r = {}
        self.name = name


class Prog:
    ENGS = ("pe", "act", "dve", "pool", "sp")

    def __init__(self, nc, stack):
        self.nc = nc
        self.stack = stack
        self.streams = {e: [] for e in self.ENGS}
        self.sem = {e: stack.enter_context(nc.semaphore("s_" + e)) for e in self.ENGS}
        self.cnt = {e: 0 for e in self.ENGS}
        self.waited = {e: {} for e in self.ENGS}
        self.dmasems = {}
        self.dmacnt = {}
        self.same_engine_sync = True

    def _waits(self, eng, reads, writes):
        deps = {}

        def add(ev):
            if ev is None:
                return
            k, v = ev
            if deps.get(k, 0) < v:
                deps[k] = v

        for b in reads:
            add(b.w)
        for b in writes:
            add(b.w)
            for k, v in b.r.items():
                add((k, v))
        out = []
        for k, v in deps.items():
            if k == eng and (eng == "pe" or not self.same_engine_sync):
                continue
            if self.waited[eng].get(k, 0) >= v:
                continue
            self.waited[eng][k] = v
            out.append((k, v))
        return out

    def _commit(self, ev, reads, writes):
        k, v = ev
        for b in writes:
            b.w = ev
            b.r = {}
        for b in reads:
            if b.r.get(k, 0) < v:
                b.r[k] = v

    def op(self, eng, fn, reads=(), writes=()):
        waits = self._waits(eng, reads, writes)
        self.cnt[eng] += 1
        ev = (eng, self.cnt[eng])
        self.streams[eng].append((waits, fn, eng, 1))
        self._commit(ev, reads, writes)

    def dma(self, eng, fn, semname, reads=(), writes=()):
        if semname not in self.dmasems:
            self.dmasems[semname] = self.stack.enter_context(self.nc.semaphore("d_" + semname))
            self.dmacnt[semname] = 0
        waits = self._waits(eng, reads, writes)
        self.dmacnt[semname] += 16
        key = "D:" + semname
        ev = (key, self.dmacnt[semname])
        self.streams[eng].append((waits, fn, key, 16))
        self._commit(ev, reads, writes)

    def semh(self, key):
        if key.startswith("D:"):
            return self.dmasems[key[2:]]
        return self.sem[key]

    def barrier(self):
        evs = [(e, self.cnt[e]) for e in self.ENGS if self.cnt[e] > 0]
        evs += [("D:" + s, c) for s, c in self.dmacnt.items() if c > 0]
        for e in self.ENGS:
            waits = []
            for k, v in evs:
                if k == e:
                    continue
                if self.waited[e].get(k, 0) >= v:
                    continue
                self.waited[e][k] = v
                waits.append((k, v))
            if waits:
                self.streams[e].append((waits, None, None, 0))

    def emit(self):
        nc = self.nc
        engobj = {"pe": "tensor", "act": "scalar", "dve": "vector", "pool": "gpsimd", "sp": "sync"}
        with nc.Block() as block:
            for e in self.ENGS:
                stream = self.streams[e]

                def body(eng, stream=stream):
                    for waits, fn, key, inc in stream:
                        for k, v in waits:
                            eng.wait_ge(self.semh(k), v)
                        if fn is not None:
                            fn(eng).then_inc(self.semh(key), inc)

                getattr(block, engobj[e])(body)


class Ring:
    def __init__(self, items):
        self.items = items
        self.i = 0

    def next(self):
        it = self.items[self.i % len(self.items)]
        self.i += 1
        return it


def pack_small(inp):
    cols = {}
    arrs = []
    off = 0

    def put(name, a):
        nonlocal off
        a = np.ascontiguousarray(a, dtype=np.float32)
        assert a.shape[0] == 128
        cols[name] = (off, a.shape[1])
        arrs.append(a)
        off += a.shape[1]

    def pm(v, nchunk):
        v = np.asarray(v, np.float32)
        lead = v.shape[:-1]
        v = v.reshape(lead + (nchunk, 128))
        v = np.moveaxis(v, -1, 0)
        return v.reshape(128, -1)

    def bc(v):
        v = np.asarray(v, np.float32).reshape(1, -1)
        return np.broadcast_to(v, (128, v.shape[1]))

    put("mixg", pm(inp["mix_norm"], 8))
    put("ffng", pm(inp["ffn_norm"], 8))
    put("fing", pm(inp["final_norm"], 8))
    put("fcw", pm(inp["ffn_conv_w"], 44))
    put("fcb", pm(inp["ffn_conv_b"], 44))
    put("scw", pm(inp["ssd_conv_w"], 32))
    put("scb", pm(inp["ssd_conv_b"], 32))
    put("sng", pm(inp["ssd_norm"], 16))
    put("dtb", bc(inp["ssd_dt_bias"]))
    put("alog", bc(inp["ssd_a_log"]))
    put("dsk", bc(inp["ssd_d"]))
    return np.concatenate(arrs, axis=1), cols


def make_consts():
    k = np.arange(128)
    c = {}
    ident = np.eye(128, dtype=np.float32)
    negU = -(k[:, None] >= k[None, :]).astype(np.float32)
    negL = -(k[:, None] < k[None, :]).astype(np.float32)
    q = np.arange(512)
    am = np.stack([np.where(r * 128 + k[:, None] < q[None, :], 0.0, NEG) for r in range(4)], axis=1)
    cb16 = np.concatenate([ident, np.ones((128, 128), np.float32), negU, negL], axis=1)
    c["identb"] = (0, 128)
    c["onesb"] = (128, 128)
    c["negUi"] = (256, 128)
    c["negLs"] = (384, 128)
    triR = (k[:, None] <= k[None, :]).astype(np.float32)
    ustr = (k[:, None] > k[None, :]).astype(np.float32)
    cmask = (k[:, None] <= k[None, :]).astype(np.float32)
    c32 = np.concatenate([triR, ustr, cmask, np.ones((128, 128), np.float32), ident], axis=1)
    f = {"triR": (0, 128), "ustr": (128, 128), "cmask": (256, 128), "ones32": (384, 128), "ident32": (512, 128)}
    import ml_dtypes
    c["_amask"] = am.reshape(128, -1).astype(ml_dtypes.bfloat16)
    return cb16.astype(ml_dtypes.bfloat16), c, c32.astype(np.float32), f


def build_program(S, plan, small_cols, n_small, cst16_cols, n_c16, cst32_cols, n_c32,
                  do_final=True, x_from_input=True):
    NT = S // TT
    nc = bass.Bass("TRN2", target_bir_lowering=False)
    xin = nc.dram_tensor("xT", [D, S], F32, kind="ExternalInput").ap()
    outT = nc.dram_tensor("outT", [D, S], F32, kind="ExternalOutput").ap()
    small_d = nc.dram_tensor("small", [128, n_small], F32, kind="ExternalInput").ap()
    c16_d = nc.dram_tensor("c16", [128, n_c16], BF16, kind="ExternalInput").ap()
    c32_d = nc.dram_tensor("c32", [128, n_c32], F32, kind="ExternalInput").ap()
    amask_d = nc.dram_tensor("amaskd", [128, 2048], BF16, kind="ExternalInput").ap()
    xres = nc.dram_tensor("xres", [D, S], F32, kind="Internal").ap()
    xin_v = xin.rearrange("(dc p) s -> p dc s", p=128)
    xres_v = xres.rearrange("(dc p) s -> p dc s", p=128)
    out_v = outT.rearrange("(dc p) s -> p dc s", p=128)

    kinds = set(p[0] for p in plan)
    wd = {}
    if "ffn" in kinds:
        wd["ffn_w_in"] = nc.dram_tensor("ffn_w_in", [4, D, 2 * DFF], F32, kind="ExternalInput").ap()
        wd["ffn_w_out"] = nc.dram_tensor("ffn_w_out", [4, DFF, D], F32, kind="ExternalInput").ap()
    if "ssd" in kinds:
        wd["ssd_w_in"] = nc.dram_tensor("ssd_w_in", [2, D, SSD_IN], F32, kind="ExternalInput").ap()
        wd["ssd_w_out"] = nc.dram_tensor("ssd_w_out", [2, SSD_DI, D], F32, kind="ExternalInput").ap()
    if "sb" in kinds:
        wd["sb_w_qkv"] = nc.dram_tensor("sb_w_qkv", [2, D, 3 * D], F32, kind="ExternalInput").ap()
        wd["sb_w_out"] = nc.dram_tensor("sb_w_out", [2, D, D], F32, kind="ExternalInput").ap()

    st = ExitStack()
    with st:
        P = Prog(nc, st)

        def sb(name, shape, dt):
            return st.enter_context(nc.sbuf_tensor(name, shape, dt))

        small = sb("small_sb", [128, n_small], F32)
        c16 = sb("c16_sb", [128, n_c16], BF16)
        c32 = sb("c32_sb", [128, n_c32], F32)
        wslots = [sb(f"wslot{i}", [128, WSLOT], BF16) for i in range(NWS)]
        wbuf = [Buf(f"wslot{i}") for i in range(NWS)]
        xts = [sb(f"xt{i}", [128, DC, TT], F32) for i in range(2)]
        xtb = [Buf(f"xt{i}") for i in range(2)]
        sqb = [sb(f"sq{i}", [128, TT], BF16) for i in range(2)]
        sqbuf = [Buf() for _ in range(2)]
        rstd = sb("rstd", [128, TT], F32)
        rstdb = Buf("rstd")
        lnv = sb("lnv", [128, TT], F32)
        lnvb = Buf("lnv")
        psums = [st.enter_context(nc.psum_tensor(f"ps{i}", [128, 512], F32)) for i in range(8)]
        psb = [Buf(f"ps{i}") for i in range(8)]
        cbuf = Buf("consts")
        xres_b = [Buf(f"xres{i}") for i in range(NT)]

        def sm(name, idx=0, n=1):
            o, _ = small_cols[name]
            return small[:, o + idx:o + idx + n]

        def k16(name):
            o, n = cst16_cols[name]
            return c16[:, o:o + n]

        def k32(name):
            o, n = cst32_cols[name]
            return c32[:, o:o + n]

        P.dma("sp", lambda e: e.dma_start(out=small[:], in_=small_d), "cst", writes=[cbuf])
        P.dma("sp", lambda e: e.dma_start(out=c16[:], in_=c16_d), "cst", writes=[cbuf])
        P.dma("sp", lambda e: e.dma_start(out=c32[:], in_=c32_d), "cst", writes=[cbuf])

        wgroups = {}

        def conv_group(key, pieces, n, parts=128):
            scr = nc.dram_tensor("scr_" + "_".join(str(k) for k in key), [parts, n], BF16, kind="Internal").ap()
            b = Buf("scr")
            i = conv_group.i % NWS
            conv_group.i += 1
            slot = wslots[i]
            for src, dstf in pieces:
                P.dma("pool", lambda e, src=src, dstf=dstf, slot=slot: e.dma_start(out=dstf(slot), in_=src),
                      f"cv{i}", writes=[wbuf[i]])
            P.dma("sp", lambda e, slot=slot, scr=scr, n=n: e.dma_start(out=scr, in_=slot[0:parts, 0:n]),
                  f"cs{i}", reads=[wbuf[i]], writes=[b])
            wgroups[key] = (scr, b, n, parts)

        conv_group.i = 0

        def v3(slot, a, b, lo=None, hi=None):
            v = slot[:, 0:a * b].rearrange("p (a b) -> p a b", a=a)
            if lo is not None:
                v = v[:, :, lo:hi]
            return v

        for item in plan:
            if item[0] == "ffn":
                l = item[1]
                Wv = wd["ffn_w_in"][l].rearrange("(dc p) n -> p dc n", p=128)
                for g in range(11):
                    conv_group(("fi", l, g), [
                        (Wv[:, :, 256 * g:256 * g + 256], lambda s: v3(s, 8, 512, 0, 256)),
                        (Wv[:, :, DFF + 256 * g:DFF + 256 * g + 256], lambda s: v3(s, 8, 512, 256, 512)),
                    ], 4096)
                Wo = wd["ffn_w_out"][l].rearrange("(fc p) n -> p fc n", p=128)
                for j in range(8):
                    conv_group(("fo", l, j), [(Wo[:, :, 128 * j:128 * j + 128], lambda s: v3(s, FC, 128))], FC * 128)
            if item[0] == "ssd":
                j = item[1]
                Wv = wd["ssd_w_in"][j].rearrange("(dc p) n -> p dc n", p=128)
                for zg in range(4):
                    conv_group(("sz", j, zg), [(Wv[:, :, 512 * zg:512 * zg + 512], lambda s: v3(s, 8, 512))], 4096)
                for xg in range(8):
                    conv_group(("sx", j, xg), [(Wv[:, :, 2048 + 512 * xg:2048 + 512 * xg + 512], lambda s: v3(s, 8, 512))], 4096)
                conv_group(("sdt", j), [(Wv[:, :, 6144:6176], lambda s: v3(s, 8, 32))], 256)
                Wo = wd["ssd_w_out"][j].rearrange("(cc p) n -> p cc n", p=128)
                for k in range(4):
                    conv_group(("sout", j, k), [(Wo[:, :, 256 * k:256 * k + 256], lambda s: v3(s, 16, 256))], 4096)
            if item[0] == "sb":
                j = item[1]
                Wv = wd["sb_w_qkv"][j].rearrange("(dc p) n -> p dc n", p=128)
                for hp in range(8):
                    conv_group(("sq", j, hp), [
                        (Wv[:, :, 128 * hp:128 * hp + 128], lambda s: v3(s, 8, 384, 0, 128)),
                        (Wv[:, :, D + 128 * hp:D + 128 * hp + 128], lambda s: v3(s, 8, 384, 128, 256)),
                        (Wv[:, :, 2 * D + 128 * hp:2 * D + 128 * hp + 128], lambda s: v3(s, 8, 384, 256, 384)),
                    ], 3072)
                Wo = wd["sb_w_out"][j].rearrange("(h p) n -> p h n", p=64)
                for jn in range(8):
                    conv_group(("so", j, jn), [
                        (Wo[:, :, 128 * jn:128 * jn + 128],
                         lambda s: s[0:64, 0:2048].rearrange("p (h n) -> p h n", h=16))], 2048, parts=64)

        wsched = []
        wstate = {"issued": 0, "used": 0}

        def w_issue_upto(n):
            while wstate["issued"] < min(n, len(wsched)):
                i = wstate["issued"]
                key = wsched[i]
                scr, b, ncol, parts = wgroups[key]
                s = i % NWS
                P.dma("sp", lambda e, s=s, scr=scr, ncol=ncol, parts=parts: e.dma_start(out=wslots[s][0:parts, 0:ncol], in_=scr),
                      f"wl{s}", reads=[b], writes=[wbuf[s]])
                wstate["issued"] += 1

        def next_w(key):
            i = wstate["used"]
            assert wsched[i] == key, (wsched[i], key)
            w_issue_upto(i + NWS)
            wstate["used"] += 1
            s = i % NWS
            return wslots[s], wbuf[s]

        for item in plan:
            if item[0] == "ffn":
                l = item[1]
                for it in range(NT):
                    wsched.extend(("fi", l, g) for g in range(11))
                    wsched.extend(("fo", l, j) for j in range(8))
            if item[0] == "ssd":
                j = item[1]
                for it in range(NT):
                    wsched.extend(("sx", j, xg) for xg in range(8))
                    wsched.extend(("sz", j, zg) for zg in range(4))
                    wsched.append(("sdt", j))
                    for blk in range(4):
                        wsched.extend(("sout", j, k) for k in range(4))
            if item[0] == "sb":
                j = item[1]
                wsched.extend(("sq", j, hp) for hp in range(8))
                for it in range(NT):
                    wsched.extend(("so", j, jn) for jn in range(8))

        xstate = {"n": 0}

        def load_x(src_v, it, srcbuf, slot=None):
            i = xstate["n"] % 2 if slot is None else slot
            xstate["n"] += 1
            P.dma("sp", lambda e, i=i, it=it: e.dma_start(out=xts[i][:], in_=src_v[:, :, it * TT:(it + 1) * TT]),
                  f"xl{i}", reads=[srcbuf] if srcbuf is not None else [], writes=[xtb[i]])
            return i

        def store_x(dst_v, it, i, dstbuf):
            P.dma("pool", lambda e, i=i, it=it: e.dma_start(out=dst_v[:, :, it * TT:(it + 1) * TT], in_=xts[i][:]),
                  f"xs{i}", reads=[xtb[i]], writes=[dstbuf] if dstbuf is not None else [])

        ps_ring = Ring([0, 1, 2, 3])
        ps_ring2 = Ring([4, 5])
        PS_NORM = 6

        def rms_stats(xi):
            xt = xts[xi]
            for dc in range(DC):
                q = dc % 2
                P.op("act", lambda e, dc=dc, q=q: e.activation(out=sqb[q][:], in_=xt[:, dc, :], func=AF.Square),
                     reads=[xtb[xi]], writes=[sqbuf[q]])
                P.op("pe", lambda e, dc=dc, q=q: e.matmul(psums[PS_NORM][:], k16("onesb"), sqb[q][:],
                                                          start=(dc == 0), stop=(dc == DC - 1)),
                     reads=[sqbuf[q], cbuf], writes=[psb[PS_NORM]])
            P.op("act", lambda e: e.activation(out=lnv[:], in_=psums[PS_NORM][:], func=AF.Ln,
                                               scale=1.0 / D, bias=sm("eps")),
                 reads=[psb[PS_NORM], cbuf], writes=[lnvb])
            P.op("act", lambda e: e.activation(out=rstd[:], in_=lnv[:], func=AF.Exp, scale=-0.5),
                 reads=[lnvb], writes=[rstdb])

        def rms_apply(xi, gname, gidx, out_t, out_b):
            xt = xts[xi]
            for dc in range(DC):
                P.op("dve", lambda e, dc=dc: e.scalar_tensor_tensor(
                    out=out_t[:, dc, :], in0=xt[:, dc, :], scalar=sm(gname, gidx + dc), in1=rstd[:],
                    op0=ALU.mult, op1=ALU.mult),
                    reads=[xtb[xi], rstdb, cbuf], writes=[out_b])

        def ffn_phase(l, src_v, srcbufs, dst_v, dstbufs):
            ph = ExitStack()
            with ph:
                hT = ph.enter_context(nc.sbuf_tensor(f"hT{l}", [128, DC, TT], BF16))
                hTb = Buf("hT")
                gT = ph.enter_context(nc.sbuf_tensor(f"gT{l}", [128, FC, TT], BF16))
                gTb = [Buf() for _ in range(FC)]
                tails = ph.enter_context(nc.sbuf_tensor(f"tails{l}", [128, 2 * FC, 2], F32))
                tailb = [Buf("tails") for _ in range(2 * FC)]
                ubs = [ph.enter_context(nc.sbuf_tensor(f"ub{l}_{i}", [128, TT + 2], F32)) for i in range(4)]
                ubb = [Buf() for _ in range(4)]
                accs = [ph.enter_context(nc.sbuf_tensor(f"acc{l}_{i}", [128, TT], F32)) for i in range(8)]
                accb = [Buf() for _ in range(8)]
                sgs = [ph.enter_context(nc.sbuf_tensor(f"sg{l}_{i}", [128, TT], F32)) for i in range(2)]
                sgb = [Buf() for _ in range(2)]
                ubr = Ring([0, 1])
                sgr = Ring([0, 1])
                P.op("pool", lambda e: e.memset(tails[:], 0.0), writes=tailb)
                cur = load_x(src_v, 0, srcbufs[0])
                for it in range(NT):
                    xi = cur
                    if it + 1 < NT:
                        cur = load_x(src_v, it + 1, srcbufs[it + 1])
                    rms_stats(xi)
                    rms_apply(xi, "ffng", l * 8, hT, hTb)
                    NCH = 44
                    cinfo = {}

                    def sPE(idx):
                        g, c = divmod(idx, 4)
                        if c == 0:
                            wt, wtb = next_w(("fi", l, g))
                            sPE.cur = (v3(wt, 8, 512), wtb)
                        wv, wtb = sPE.cur
                        pi = ps_ring.next()
                        cinfo[idx] = pi
                        for dc in range(DC):
                            P.op("pe", lambda e, dc=dc, c=c, pi=pi, wv=wv: e.matmul(
                                psums[pi][:], wv[:, dc, c * 128:(c + 1) * 128], hT[:, dc, :],
                                start=(dc == 0), stop=(dc == DC - 1)),
                                reads=[wtb, hTb], writes=[psb[pi]])

                    def chan(idx):
                        g, c = divmod(idx, 4)
                        return 2 * g + c if c < 2 else FC + 2 * g + (c - 2)

                    def sIn(idx):
                        ch = chan(idx)
                        pi = cinfo[idx]
                        ui = idx % 4
                        ub = ubs[ui]
                        P.op("dve", lambda e, ub=ub, ch=ch: e.tensor_copy(out=ub[:, 0:2], in_=tails[:, ch, :]),
                             reads=[tailb[ch]], writes=[ubb[ui]])
                        P.op("act", lambda e, ub=ub, pi=pi: e.activation(out=ub[:, 2:TT + 2], in_=psums[pi][:], func=AF.Copy),
                             reads=[psb[pi]], writes=[ubb[ui]])

                    def sMid(idx):
                        ch = chan(idx)
                        ui = idx % 4
                        ub = ubs[ui]
                        ai = idx % 8
                        acc = accs[ai]
                        P.op("dve", lambda e, ub=ub, ch=ch: e.tensor_copy(out=tails[:, ch, :], in_=ub[:, TT:TT + 2]),
                             reads=[ubb[ui]], writes=[tailb[ch]])
                        w2 = sm("fcw", (l * 3 + 2) * 44 + ch)
                        bb = sm("fcb", l * 44 + ch)
                        P.op("pool", lambda e, ub=ub, acc=acc, w2=w2, bb=bb: e.tensor_scalar(
                            out=acc[:], in0=ub[:, 2:TT + 2], scalar1=w2, scalar2=bb, op0=ALU.mult, op1=ALU.add),
                            reads=[ubb[ui], cbuf], writes=[accb[ai]])

                    def sSTT(idx):
                        ch = chan(idx)
                        ui = idx % 4
                        ub = ubs[ui]
                        ai = idx % 8
                        acc = accs[ai]
                        w0 = sm("fcw", (l * 3 + 0) * 44 + ch)
                        w1 = sm("fcw", (l * 3 + 1) * 44 + ch)
                        P.op("dve", lambda e, ub=ub, acc=acc, w1=w1: e.scalar_tensor_tensor(
                            out=acc[:], in0=ub[:, 1:TT + 1], scalar=w1, in1=acc[:], op0=ALU.mult, op1=ALU.add),
                            reads=[ubb[ui], accb[ai], cbuf], writes=[accb[ai]])
                        P.op("dve", lambda e, ub=ub, acc=acc, w0=w0: e.scalar_tensor_tensor(
                            out=acc[:], in0=ub[:, 0:TT], scalar=w0, in1=acc[:], op0=ALU.mult, op1=ALU.add),
                            reads=[ubb[ui], accb[ai], cbuf], writes=[accb[ai]])

                    def sGate(g):
                        for c in range(2):
                            si = sgr.next()
                            fi = 2 * g + c
                            ga = (4 * g + c) % 8
                            ua = (4 * g + 2 + c) % 8
                            P.op("act", lambda e, si=si, ga=ga: e.activation(out=sgs[si][:], in_=accs[ga][:], func=AF.Silu),
                                 reads=[accb[ga]], writes=[sgb[si]])
                            P.op("pool", lambda e, si=si, ua=ua, fi=fi: e.tensor_tensor(
                                out=gT[:, fi, :], in0=sgs[si][:], in1=accs[ua][:], op=ALU.mult),
                                reads=[sgb[si], accb[ua]], writes=[gTb[fi]])

                    for i in range(-2, NCH + 6):
                        if 0 <= i - 5 and (i - 5) % 4 == 0 and (i - 5) // 4 < 11:
                            sGate((i - 5) // 4)
                        if 0 <= i + 2 < NCH:
                            sPE(i + 2)
                        if 0 <= i + 1 < NCH:
                            sIn(i + 1)
                        if 0 <= i < NCH:
                            sMid(i)
                        if 0 <= i - 1 < NCH:
                            sSTT(i - 1)
                    xt = xts[xi]
                    for j in range(8):
                        wt, wtb = next_w(("fo", l, j))
                        wv = v3(wt, FC, 128)
                        pi = ps_ring2.next()
                        for fc in range(FC):
                            P.op("pe", lambda e, fc=fc, pi=pi, wv=wv: e.matmul(
                                psums[pi][:], wv[:, fc, :], gT[:, fc, :], start=(fc == 0), stop=(fc == FC - 1)),
                                reads=[wtb, gTb[fc]], writes=[psb[pi]])
                        P.op("dve", lambda e, j=j, pi=pi, xt=xt: e.tensor_tensor(
                            out=xt[:, j, :], in0=psums[pi][:], in1=xt[:, j, :], op=ALU.add),
                            reads=[psb[pi], xtb[xi]], writes=[xtb[xi]])
                    store_x(dst_v, it, xi, dstbufs[it])
                P.barrier()

        def sb_phase(j, l, src_v, srcbufs, dst_v, dstbufs):
            NB = S // 128
            NG = S // 512
            oscr = nc.dram_tensor(f"oscr{j}", [SB_H, 64, S], BF16, kind="Internal").ap()
            oscr_v = oscr.rearrange("h p s -> p h s")
            oscrb = [Buf() for _ in range(NG)]
            ph = ExitStack()
            with ph:
                def al(name, shape, dt):
                    return ph.enter_context(nc.sbuf_tensor(f"{name}_{j}", shape, dt))
                amask = al("amask", [128, 2048], BF16)
                amaskb = Buf()
                P.dma("sp", lambda e: e.dma_start(out=amask[:], in_=amask_d), "cst", writes=[amaskb])
                hTall = al("hTall", [128, DC, S], BF16)
                hTallb = [Buf() for _ in range(NT)]
                cur = load_x(src_v, 0, srcbufs[0])
                for it in range(NT):
                    xi = cur
                    if it + 1 < NT:
                        cur = load_x(src_v, it + 1, srcbufs[it + 1])
                    rms_stats(xi)
                    rms_apply(xi, "mixg", l * 8, hTall[:, :, it * TT:(it + 1) * TT], hTallb[it])
                QT = al("QT", [128, S], BF16)
                KT = al("KT", [128, S], BF16)
                V = al("V", [128, NB, 128], BF16)
                QTb = [Buf() for _ in range(NT)]
                KTb = [Buf() for _ in range(NT)]
                Vb = [Buf() for _ in range(NT)]
                e_ = [[al(f"e{c}{q}", [128, 512], F32) for q in range(2)] for c in range(2)]
                spb_ = [[al(f"spb{c}{q}", [128, 512], BF16) for q in range(2)] for c in range(2)]
                ex_ = [[al(f"ex{c}{q}", [128, 512], BF16) for q in range(2)] for c in range(2)]
                Wb_ = [[al(f"Wb{c}{q}", [128, 512], BF16) for q in range(2)] for c in range(2)]
                osb_ = [al(f"osb{c}", [64, 512], BF16) for c in range(2)]
                eB = [[Buf() for q in range(2)] for c in range(2)]
                spbB = [[Buf() for q in range(2)] for c in range(2)]
                exB = [[Buf() for q in range(2)] for c in range(2)]
                WbB = [[Buf() for q in range(2)] for c in range(2)]
                osbB = [Buf() for _ in range(2)]
                SBK = [0, 1]
                CBK = [2, 3]
                OBK = [[4, 5], [6, 7]]
                ps_ring_sb = Ring([0, 1])
                for hp in range(8):
                    wt, wtb = next_w(("sq", j, hp))
                    wv = v3(wt, 8, 384)
                    for it in range(NT):
                        tsl = slice(it * TT, (it + 1) * TT)
                        pi = ps_ring_sb.next()
                        for dc in range(DC):
                            P.op("pe", lambda e, dc=dc, pi=pi, wv=wv, tsl=tsl: e.matmul(
                                psums[pi][:], wv[:, dc, 0:128], hTall[:, dc, tsl], start=(dc == 0), stop=(dc == DC - 1)),
                                reads=[wtb, hTallb[it]], writes=[psb[pi]])
                        P.op("act", lambda e, pi=pi, tsl=tsl: e.activation(out=QT[:, tsl], in_=psums[pi][:], func=AF.Copy),
                             reads=[psb[pi]], writes=[QTb[it]])
                        pi = ps_ring_sb.next()
                        for dc in range(DC):
                            P.op("pe", lambda e, dc=dc, pi=pi, wv=wv, tsl=tsl: e.matmul(
                                psums[pi][:], wv[:, dc, 128:256], hTall[:, dc, tsl], start=(dc == 0), stop=(dc == DC - 1)),
                                reads=[wtb, hTallb[it]], writes=[psb[pi]])
                        P.op("dve", lambda e, pi=pi, tsl=tsl: e.tensor_copy(out=KT[:, tsl], in_=psums[pi][:]),
                             reads=[psb[pi]], writes=[KTb[it]])
                        pi = ps_ring_sb.next()
                        for blk in range(4):
                            t0 = it * TT + blk * 128
                            for dc in range(DC):
                                P.op("pe", lambda e, dc=dc, pi=pi, wv=wv, t0=t0, blk=blk: e.matmul(
                                    psums[pi][:, blk * 128:(blk + 1) * 128], hTall[:, dc, t0:t0 + 128], wv[:, dc, 256:384],
                                    start=(dc == 0), stop=(dc == DC - 1)),
                                    reads=[wtb, hTallb[it]], writes=[psb[pi]])
                        P.op("act", lambda e, pi=pi, it=it: e.activation(
                            out=V[:, 4 * it:4 * it + 4, :], in_=psums[pi][:].rearrange("p (b n) -> p b n", b=4), func=AF.Copy),
                            reads=[psb[pi]], writes=[Vb[it]])
                    steps = [(g, jb) for g in range(NG) for jb in range(4 * g + 3, -1, -1)]
                    nst = len(steps)

                    def st_info(i):
                        g, jb = steps[i]
                        return g, jb, jb == 4 * g + 3, jb == 0, jb - 4 * g

                    def stA(i):
                        g, jb, first, last, r = st_info(i)
                        ksl = slice(jb * 128, (jb + 1) * 128)
                        qsl = slice(g * 512, (g + 1) * 512)
                        for c in range(2):
                            lo = 64 * c
                            P.op("pe", lambda e, c=c, lo=lo, ksl=ksl, qsl=qsl, r=r: e.matmul(
                                psums[SBK[c]][:], KT[lo:lo + 64, ksl], QT[lo:lo + 64, qsl], start=True, stop=(r < 0)),
                                reads=[KTb[jb // 4], QTb[g]], writes=[psb[SBK[c]]])
                            if r >= 0:
                                P.op("pe", lambda e, c=c, r=r: e.matmul(
                                    psums[SBK[c]][:], k16("identb"), amask[:, r * 512:(r + 1) * 512], start=False, stop=True),
                                    reads=[cbuf, amaskb], writes=[psb[SBK[c]]])

                    def stBC(i):
                        q = i % 2
                        for c in range(2):
                            P.op("act", lambda e, c=c, q=q: e.activation(
                                out=e_[c][q][:], in_=psums[SBK[c]][:], func=AF.Exp, scale=SB_DH ** -0.5),
                                reads=[psb[SBK[c]]], writes=[eB[c][q]])
                        for c in range(2):
                            P.op("act", lambda e, c=c, q=q: e.activation(
                                out=spb_[c][q][:], in_=e_[c][q][:], func=AF.Ln, bias=sm("one")),
                                reads=[eB[c][q], cbuf], writes=[spbB[c][q]])

                    def stD(i):
                        g, jb, first, last, r = st_info(i)
                        q = i % 2
                        for c in range(2):
                            P.op("pe", lambda e, c=c, q=q, first=first: e.matmul(
                                psums[CBK[c]][:], k16("negUi"), spb_[c][q][:], start=first, stop=True, skip_group_check=True),
                                reads=[spbB[c][q], cbuf], writes=[psb[CBK[c]]])

                    def stE(i):
                        q = i % 2
                        for c in range(2):
                            P.op("act", lambda e, c=c, q=q: e.activation(out=ex_[c][q][:], in_=psums[CBK[c]][:], func=AF.Exp),
                                 reads=[psb[CBK[c]]], writes=[exB[c][q]])

                    def stF(i):
                        g, jb, first, last, r = st_info(i)
                        q = i % 2
                        if last:
                            return
                        for c in range(2):
                            P.op("pe", lambda e, c=c, q=q: e.matmul(
                                psums[CBK[c]][:], k16("negLs"), spb_[c][q][:], start=False, stop=True, skip_group_check=True),
                                reads=[spbB[c][q], cbuf], writes=[psb[CBK[c]]])

                    def stG(i):
                        q = i % 2
                        for c in range(2):
                            P.op("dve" if c == 0 else "pool", lambda e, c=c, q=q: e.tensor_tensor(
                                out=Wb_[c][q][:], in0=e_[c][q][:], in1=ex_[c][q][:], op=ALU.mult),
                                reads=[eB[c][q], exB[c][q]], writes=[WbB[c][q]])

                    def stH(i):
                        g, jb, first, last, r = st_info(i)
                        q = i % 2
                        ob = g % 2
                        for c in range(2):
                            P.op("pe", lambda e, c=c, q=q, jb=jb, first=first, last=last, ob=ob: e.matmul(
                                psums[OBK[c][ob]][0:64, :], V[:, jb, 64 * c:64 * c + 64], Wb_[c][q][:], start=first, stop=last),
                                reads=[Vb[jb // 4], WbB[c][q]], writes=[psb[OBK[c][ob]]])
                        if last:
                            qsl = slice(g * 512, (g + 1) * 512)
                            for c in range(2):
                                P.op("dve" if c == 1 else "act", (lambda e, c=c, ob=ob: e.tensor_copy(
                                    out=osb_[c][:], in_=psums[OBK[c][ob]][0:64, :])) if c == 1 else (lambda e, c=c, ob=ob: e.activation(
                                    out=osb_[c][:], in_=psums[OBK[c][ob]][0:64, :], func=AF.Copy)),
                                    reads=[psb[OBK[c][ob]]], writes=[osbB[c]])
                                P.dma("sp", lambda e, c=c, hp=hp, qsl=qsl: e.dma_start(out=oscr[2 * hp + c][:, qsl], in_=osb_[c][:]),
                                      f"ov{c}", reads=[osbB[c]], writes=[oscrb[g]])

                    for i in range(-1, nst + 1):
                        if i + 1 < nst:
                            stA(i + 1)
                        if 0 <= i - 1 < nst:
                            stF(i - 1)
                            stH(i - 1)
                        if 0 <= i < nst:
                            stD(i)
                        if i + 1 < nst:
                            stBC(i + 1)
                        if 0 <= i < nst:
                            stE(i)
                            stG(i)
                P.barrier()
            ph = ExitStack()
            with ph:
                ots = [ph.enter_context(nc.sbuf_tensor(f"otile{j}_{i}", [64, SB_H, TT], BF16)) for i in range(2)]
                otB = [Buf() for _ in range(2)]
                cur = load_x(src_v, 0, srcbufs[0])
                for it in range(NT):
                    xi = cur
                    if it + 1 < NT:
                        cur = load_x(src_v, it + 1, srcbufs[it + 1])
                    oi = it % 2
                    P.dma("sp", lambda e, oi=oi, it=it: e.dma_start(out=ots[oi][:], in_=oscr_v[:, :, it * TT:(it + 1) * TT]),
                          f"ol{oi}", reads=[oscrb[it]], writes=[otB[oi]])
                    xt = xts[xi]
                    for jn in range(8):
                        wt, wtb = next_w(("so", j, jn))
                        wv = wt[0:64, 0:2048].rearrange("p (h n) -> p h n", h=SB_H)
                        pi = ps_ring2.next()
                        for h in range(SB_H):
                            P.op("pe", lambda e, h=h, pi=pi, wv=wv, oi=oi: e.matmul(
                                psums[pi][:], wv[:, h, :], ots[oi][:, h, :], start=(h == 0), stop=(h == SB_H - 1)),
                                reads=[wtb, otB[oi]], writes=[psb[pi]])
                        P.op("dve", lambda e, jn=jn, pi=pi, xt=xt: e.tensor_tensor(
                            out=xt[:, jn, :], in0=psums[pi][:], in1=xt[:, jn, :], op=ALU.add),
                            reads=[psb[pi], xtb[xi]], writes=[xtb[xi]])
                    store_x(dst_v, it, xi, dstbufs[it])
                P.barrier()

        def ssd_phase(j, l, src_v, srcbufs, dst_v, dstbufs):
            ph = ExitStack()
            with ph:
                def al(name, shape, dt):
                    return ph.enter_context(nc.sbuf_tensor(f"{name}_s{j}", shape, dt))
                hT = al("hT", [128, DC, TT], BF16)
                hTb = Buf()
                state = al("state", [128, 2048], F32)
                stateb = Buf()
                sbf = al("sbf", [128, 2048], BF16)
                sbfb = Buf()
                tails = al("tails", [128, 32, 3], F32)
                tailb = [Buf() for _ in range(32)]
                negA = al("negA", [128, 32], F32)
                negAb = Buf()
                ubs = [al(f"ub{i}", [128, TT + 3], F32) for i in range(4)]
                ubb = [Buf() for _ in range(4)]
                accs = [al(f"acc{i}", [128, TT], F32) for i in range(4)]
                accb = [Buf() for _ in range(4)]
                css = [al(f"cs{i}", [128, TT], F32) for i in range(2)]
                csb = [Buf() for _ in range(2)]
                xs_tok = al("xs_tok", [128, 4, 2048], BF16)
                xs_tokb = [Buf() for _ in range(16)]
                B_tok = al("B_tok", [128, 4, 1024], BF16)
                B_tokb = [Buf() for _ in range(8)]
                BT = al("BT", [128, 8, TT], BF16)
                BTb = [Buf() for _ in range(8)]
                CT = al("CT", [128, 8, TT], BF16)
                CTb = [Buf() for _ in range(8)]
                sz = al("sz", [128, 4, 2048], BF16)
                szb = [Buf() for _ in range(4)]
                dtx = al("dtx", [128, 128], F32)
                dtxb = Buf()
                dtt = al("dtt", [128, 128], F32)
                dttb = Buf()
                at = al("at", [128, 128], F32)
                atb = Buf()
                acs = al("acs", [128, 64], F32)
                acsb = Buf()
                eall = al("eall", [128, 64], F32)
                eallb = Buf()
                dlt = al("dlt", [128, 32], F32)
                dltb = Buf()
                wsc = al("wsc", [128, 32], F32)
                wscb = Buf()
                decs = [al(f"dec{i}", [128, 512], F32) for i in range(3)]
                decb = [Buf() for _ in range(3)]
                cbms = [al(f"cbm{i}", [128, 128], F32) for i in range(3)]
                cbmb = [Buf() for _ in range(3)]
                scs = [al(f"sc{i}", [128, 512], BF16) for i in range(3)]
                scb_ = [Buf() for _ in range(3)]
                pcb = [Buf() for _ in range(4)]
                xdt = al("xdt", [128, 2048], BF16)
                xdtb = Buf()
                xw = al("xw", [128, 2048], BF16)
                xwb = Buf()
                xsD = al("xsD", [128, 2048], BF16)
                xsDb = Buf()
                yscs = [al(f"ysc{i}", [128, 256], F32) for i in range(2)]
                yscb = [Buf() for _ in range(2)]
                y = al("y", [128, 2048], F32)
                yb = Buf()
                ss = al("ss", [128, 1], F32)
                lnss = al("lnss", [128, 1], F32)
                rs = al("rs", [128, 1], F32)
                ssb, lnssb, rsb = Buf(), Buf(), Buf()
                yTc = al("yTc", [128, 16, 128], BF16)
                yTcb = Buf()
                LA = xts[1][:].rearrange("p a (b c) -> p (a b) c", c=128)
                LAb = [Buf(), Buf()]
                ident32 = k32("ident32")
                triR = k32("triR")
                rr = Ring([0, 1])
                rr2 = Ring([0, 1])
                rr3 = Ring([0, 1])

                P.op("pool", lambda e: e.memset(tails[:], 0.0), writes=tailb)
                P.op("pool", lambda e: e.memset(state[:], 0.0), writes=[stateb])
                P.op("pool", lambda e: e.memset(sbf[:], 0.0), writes=[sbfb])
                P.op("act", lambda e: e.activation(out=negA[:], in_=sm("alog", j * 32, 32), func=AF.Exp),
                     reads=[cbuf], writes=[negAb])
                P.op("dve", lambda e: e.tensor_scalar(out=negA[:], in0=negA[:], scalar1=-1.0, scalar2=None, op0=ALU.mult),
                     reads=[negAb], writes=[negAb])

                for it in range(NT):
                    xi = load_x(src_v, it, srcbufs[it], slot=0)
                    xt = xts[xi]
                    rms_stats(xi)
                    rms_apply(xi, "mixg", l * 8, hT, hTb)
                    cinfo = {}

                    def cPE(ch):
                        xg, c = divmod(ch, 4)
                        if c == 0:
                            wt, wtb = next_w(("sx", j, xg))
                            cPE.cur = (v3(wt, 8, 512), wtb)
                        wv, wtb = cPE.cur
                        pi = ps_ring.next()
                        cinfo[ch] = pi
                        for dc in range(DC):
                            P.op("pe", lambda e, dc=dc, c=c, pi=pi, wv=wv: e.matmul(
                                psums[pi][:], wv[:, dc, c * 128:(c + 1) * 128], hT[:, dc, :],
                                start=(dc == 0), stop=(dc == DC - 1)),
                                reads=[wtb, hTb], writes=[psb[pi]])

                    def cIn(ch):
                        pi = cinfo[ch]
                        ui = ch % 4
                        ub = ubs[ui]
                        P.op("dve", lambda e, ub=ub, ch=ch: e.tensor_copy(out=ub[:, 0:3], in_=tails[:, ch, :]),
                             reads=[tailb[ch]], writes=[ubb[ui]])
                        P.op("act", lambda e, ub=ub, pi=pi: e.activation(out=ub[:, 3:TT + 3], in_=psums[pi][:], func=AF.Copy),
                             reads=[psb[pi]], writes=[ubb[ui]])

                    def cMid(ch):
                        ui = ch % 4
                        ub = ubs[ui]
                        acc = accs[ui]
                        P.op("dve", lambda e, ub=ub, ch=ch: e.tensor_copy(out=tails[:, ch, :], in_=ub[:, TT:TT + 3]),
                             reads=[ubb[ui]], writes=[tailb[ch]])
                        w3 = sm("scw", (j * 4 + 3) * 32 + ch)
                        bb = sm("scb", j * 32 + ch)
                        P.op("pool", lambda e, ub=ub, acc=acc, w3=w3, bb=bb: e.tensor_scalar(
                            out=acc[:], in0=ub[:, 3:TT + 3], scalar1=w3, scalar2=bb, op0=ALU.mult, op1=ALU.add),
                            reads=[ubb[ui], cbuf], writes=[accb[ui]])

                    def cSTT(ch):
                        ui = ch % 4
                        ub = ubs[ui]
                        acc = accs[ui]
                        for k in range(3):
                            wk = sm("scw", (j * 4 + k) * 32 + ch)
                            P.op("dve", lambda e, ub=ub, acc=acc, wk=wk, k=k: e.scalar_tensor_tensor(
                                out=acc[:], in0=ub[:, k:TT + k], scalar=wk, in1=acc[:], op0=ALU.mult, op1=ALU.add),
                                reads=[ubb[ui], accb[ui], cbuf], writes=[accb[ui]])

                    def cAct(ch):
                        ui = ch % 4
                        acc = accs[ui]
                        if ch < 24:
                            ci = ch % 2
                            cs = css[ci]
                            P.op("act", lambda e, acc=acc, cs=cs: e.activation(out=cs[:], in_=acc[:], func=AF.Silu),
                                 reads=[accb[ui]], writes=[csb[ci]])
                            if ch >= 16:
                                g = ch - 16
                                P.op("pool", lambda e, g=g, cs=cs: e.tensor_copy(out=BT[:, g, :], in_=cs[:]),
                                     reads=[csb[ci]], writes=[BTb[g]])
                            pt = ps_ring.next()
                            for blk in range(4):
                                P.op("pe", lambda e, blk=blk, pt=pt, cs=cs: e.transpose(
                                    psums[pt][:, blk * 128:(blk + 1) * 128], cs[:, blk * 128:(blk + 1) * 128], ident32),
                                    reads=[csb[ci], cbuf], writes=[psb[pt]])
                            pv = psums[pt][:].rearrange("p (b n) -> p b n", b=4)
                            if ch < 16:
                                P.op("dve", lambda e, pv=pv, ch=ch: e.tensor_copy(
                                    out=xs_tok[:, :, ch * 128:(ch + 1) * 128], in_=pv),
                                    reads=[psb[pt]], writes=[xs_tokb[ch]])
                            else:
                                g = ch - 16
                                P.op("dve", lambda e, pv=pv, g=g: e.tensor_copy(
                                    out=B_tok[:, :, g * 128:(g + 1) * 128], in_=pv),
                                    reads=[psb[pt]], writes=[B_tokb[g]])
                        else:
                            g = ch - 24
                            P.op("act", lambda e, acc=acc, g=g: e.activation(out=CT[:, g, :], in_=acc[:], func=AF.Silu),
                                 reads=[accb[ui]], writes=[CTb[g]])

                    for i in range(-2, 32 + 2):
                        if 0 <= i + 2 < 32:
                            cPE(i + 2)
                        if 0 <= i + 1 < 32:
                            cIn(i + 1)
                        if 0 <= i < 32:
                            cMid(i)
                        if 0 <= i - 1 < 32:
                            cSTT(i - 1)
                        if 0 <= i - 2 < 32:
                            cAct(i - 2)
                    for zg in range(4):
                        wt, wtb = next_w(("sz", j, zg))
                        wv = v3(wt, 8, 512)
                        for blk in range(4):
                            pi = ps_ring.next()
                            for dc in range(DC):
                                P.op("pe", lambda e, dc=dc, blk=blk, pi=pi, wv=wv: e.matmul(
                                    psums[pi][:], hT[:, dc, blk * 128:(blk + 1) * 128], wv[:, dc, :],
                                    start=(dc == 0), stop=(dc == DC - 1)),
                                    reads=[wtb, hTb], writes=[psb[pi]])
                            P.op("act", lambda e, blk=blk, zg=zg, pi=pi: e.activation(
                                out=sz[:, blk, zg * 512:(zg + 1) * 512], in_=psums[pi][:], func=AF.Silu),
                                reads=[psb[pi]], writes=[szb[blk]])
                    wt, wtb = next_w(("sdt", j))
                    wv = v3(wt, 8, 32)
                    pi = ps_ring.next()
                    for blk in range(4):
                        for dc in range(DC):
                            P.op("pe", lambda e, dc=dc, blk=blk, pi=pi, wv=wv: e.matmul(
                                psums[pi][:, blk * 32:(blk + 1) * 32], hT[:, dc, blk * 128:(blk + 1) * 128], wv[:, dc, :],
                                start=(dc == 0), stop=(dc == DC - 1)),
                                reads=[wtb, hTb], writes=[psb[pi]])
                    dtb_bc = sm("dtb", j * 32, 32).unsqueeze(1).to_broadcast([128, 4, 32])
                    P.op("dve", lambda e, pi=pi: e.tensor_tensor(
                        out=dtx[:].rearrange("p (b h) -> p b h", b=4), in0=psums[pi][:, 0:128].rearrange("p (b h) -> p b h", b=4),
                        in1=dtb_bc, op=ALU.add),
                        reads=[psb[pi], cbuf], writes=[dtxb])
                    P.op("act", lambda e: e.activation(out=dtx[:], in_=dtx[:], func=AF.Exp), reads=[dtxb], writes=[dtxb])
                    P.op("act", lambda e: e.activation(out=dtt[:], in_=dtx[:], func=AF.Ln, bias=sm("one")),
                         reads=[dtxb, cbuf], writes=[dttb])
                    P.op("dve", lambda e: e.tensor_tensor(
                        out=at[:].rearrange("p (b h) -> p b h", b=4), in0=dtt[:].rearrange("p (b h) -> p b h", b=4),
                        in1=negA[:].unsqueeze(1).to_broadcast([128, 4, 32]), op=ALU.mult),
                        reads=[dttb, negAb], writes=[atb])
                    PYR = [4, 5, 6]
                    PCB = 7

                    def prep(blk):
                        a_blk = at[:, blk * 32:(blk + 1) * 32]
                        dt_blk = dtt[:, blk * 32:(blk + 1) * 32]
                        pst = ps_ring.next()
                        P.op("pe", lambda e, pst=pst, a_blk=a_blk: e.matmul(psums[pst][:, 0:32], triR, a_blk, start=True, stop=True),
                             reads=[atb, cbuf], writes=[psb[pst]])
                        P.op("pe", lambda e, pst=pst, a_blk=a_blk: e.matmul(psums[pst][:, 32:64], k32("ones32"), a_blk, start=True, stop=True),
                             reads=[atb, cbuf], writes=[psb[pst]])
                        P.op("act", lambda e, pst=pst: e.activation(out=acs[:], in_=psums[pst][:, 0:64], func=AF.Copy),
                             reads=[psb[pst]], writes=[acsb])
                        P.op("act", lambda e: e.activation(out=eall[:], in_=acs[:], func=AF.Exp), reads=[acsb], writes=[eallb])
                        P.op("dve", lambda e: e.tensor_tensor(out=dlt[:], in0=acs[:, 32:64], in1=acs[:, 0:32], op=ALU.subtract),
                             reads=[acsb], writes=[dltb])
                        P.op("act", lambda e: e.activation(out=wsc[:], in_=dlt[:], func=AF.Exp), reads=[dltb], writes=[wscb])
                        for half, eng in ((0, "pool"), (1, "dve")):
                            hs = slice(16 * half, 16 * half + 16)
                            P.op(eng, lambda e, a_blk=a_blk, hs=hs: e.tensor_tensor(
                                out=LA[:, hs, :], in0=k32("ustr").unsqueeze(1).to_broadcast([128, 16, 128]),
                                in1=a_blk[:, hs].unsqueeze(2).to_broadcast([128, 16, 128]), op=ALU.mult),
                                reads=[atb, cbuf], writes=[LAb[half]])
                        xs_v = xs_tok[:, blk, :].rearrange("p (h d) -> p h d", h=32)
                        P.op("pool", lambda e, xs_v=xs_v, dt_blk=dt_blk: e.tensor_tensor(
                            out=xdt[:].rearrange("p (h d) -> p h d", h=32), in0=xs_v,
                            in1=dt_blk.unsqueeze(2).to_broadcast([128, 32, 64]), op=ALU.mult),
                            reads=xs_tokb + [dttb], writes=[xdtb])
                        P.op("dve", lambda e: e.tensor_tensor(
                            out=xw[:].rearrange("p (h d) -> p h d", h=32), in0=xdt[:].rearrange("p (h d) -> p h d", h=32),
                            in1=wsc[:].unsqueeze(2).to_broadcast([128, 32, 64]), op=ALU.mult),
                            reads=[xdtb, wscb], writes=[xwb])
                        P.op("pool", lambda e, xs_v=xs_v: e.tensor_tensor(
                            out=xsD[:].rearrange("p (h d) -> p h d", h=32), in0=xs_v,
                            in1=sm("dsk", j * 32, 32).unsqueeze(2).to_broadcast([128, 32, 64]), op=ALU.mult),
                            reads=xs_tokb + [cbuf], writes=[xsDb])

                    def groups(blk):
                        csl = slice(blk * 128, (blk + 1) * 128)
                        ginfo = {}

                        def gA(g):
                            pg = ps_ring.next()
                            for hh in range(4):
                                P.op("pe", lambda e, pg=pg, hh=hh, g=g: e.matmul(
                                    psums[pg][:, hh * 128:(hh + 1) * 128], LA[:, 4 * g + hh, :], triR, start=True, stop=True),
                                    reads=[LAb[g // 4], cbuf], writes=[psb[pg]])
                            q = g % 4
                            P.op("pe", lambda e, g=g, q=q: e.matmul(
                                psums[PCB][:, q * 128:(q + 1) * 128], BT[:, g, csl], CT[:, g, csl], start=True, stop=True),
                                reads=[BTb[g], CTb[g]], writes=[psb[PCB]])
                            ginfo[g] = pg

                        def gB(g):
                            pg = ginfo[g]
                            di = g % 3
                            q = g % 4
                            P.op("act", lambda e, pg=pg, di=di: e.activation(out=decs[di][:], in_=psums[pg][:], func=AF.Exp),
                                 reads=[psb[pg]], writes=[decb[di]])
                            P.op("dve", lambda e, q=q, di=di: e.tensor_tensor(
                                out=cbms[di][:], in0=psums[PCB][:, q * 128:(q + 1) * 128], in1=k32("cmask"), op=ALU.mult),
                                reads=[psb[PCB], cbuf], writes=[cbmb[di]])

                        def gC(g):
                            di = g % 3
                            P.op("dve", lambda e, di=di: e.tensor_tensor(
                                out=scs[di][:].rearrange("p (h t) -> p h t", h=4), in0=decs[di][:].rearrange("p (h t) -> p h t", h=4),
                                in1=cbms[di][:].unsqueeze(1).to_broadcast([128, 4, 128]), op=ALU.mult),
                                reads=[decb[di], cbmb[di]], writes=[scb_[di]])

                        def gD(g):
                            di = g % 3
                            py = PYR[g % 3]
                            for hh in range(4):
                                h = 4 * g + hh
                                P.op("pe", lambda e, py=py, hh=hh, h=h, di=di: e.matmul(
                                    psums[py][:, hh * 64:(hh + 1) * 64], scs[di][:, hh * 128:(hh + 1) * 128],
                                    xdt[:, h * 64:(h + 1) * 64], start=(hh == 0), stop=False, skip_group_check=True),
                                    reads=[scb_[di], xdtb], writes=[psb[py]])
                            P.op("pe", lambda e, py=py, g=g: e.matmul(
                                psums[py][:, 0:256], k16("identb"), xsD[:, g * 256:(g + 1) * 256], start=False, stop=True,
                                skip_group_check=True),
                                reads=[xsDb, cbuf], writes=[psb[py]])
                            P.op("pe", lambda e, py=py, g=g: e.matmul(
                                psums[py][:, 256:512], CT[:, g, csl], sbf[:, g * 256:(g + 1) * 256], start=False, stop=True,
                                skip_group_check=True),
                                reads=[CTb[g], sbfb], writes=[psb[py]])

                        def gE(g):
                            py = PYR[g % 3]
                            yi = g % 2
                            P.op("dve", lambda e, py=py, g=g, yi=yi: e.tensor_tensor(
                                out=yscs[yi][:].rearrange("p (h d) -> p h d", h=4),
                                in0=psums[py][:, 256:512].rearrange("p (h d) -> p h d", h=4),
                                in1=eall[:, 4 * g:4 * g + 4].unsqueeze(2).to_broadcast([128, 4, 64]), op=ALU.mult),
                                reads=[psb[py], eallb], writes=[yscb[yi]])
                            P.op("dve", lambda e, py=py, g=g, yi=yi: e.tensor_tensor(
                                out=y[:, g * 256:(g + 1) * 256], in0=psums[py][:, 0:256], in1=yscs[yi][:], op=ALU.add),
                                reads=[psb[py], yscb[yi]], writes=[yb])

                        for i in range(-2, 8 + 2):
                            if 0 <= i + 2 < 8:
                                gA(i + 2)
                            if 0 <= i + 1 < 8:
                                gB(i + 1)
                            if 0 <= i < 8:
                                gC(i)
                            if 0 <= i - 1 < 8:
                                gD(i - 1)
                            if 0 <= i - 2 < 8:
                                gE(i - 2)

                    def state_update(blk):
                        P.op("pool", lambda e: e.tensor_tensor(
                            out=state[:].rearrange("p (h d) -> p h d", h=32), in0=state[:].rearrange("p (h d) -> p h d", h=32),
                            in1=eall[:, 32:64].unsqueeze(2).to_broadcast([128, 32, 64]), op=ALU.mult),
                            reads=[stateb, eallb], writes=[stateb])
                        for g2 in range(4):
                            pu = ps_ring.next()
                            for q in range(2):
                                g = 2 * g2 + q
                                P.op("pe", lambda e, pu=pu, q=q, g=g, blk=blk: e.matmul(
                                    psums[pu][:, q * 256:(q + 1) * 256], B_tok[:, blk, g * 128:(g + 1) * 128],
                                    xw[:, g * 256:(g + 1) * 256], start=True, stop=True),
                                    reads=[B_tokb[g], xwb], writes=[psb[pu]])
                            P.op("dve", lambda e, pu=pu, g2=g2: e.tensor_tensor(
                                out=state[:, g2 * 512:(g2 + 1) * 512], in0=psums[pu][:], in1=state[:, g2 * 512:(g2 + 1) * 512], op=ALU.add),
                                reads=[psb[pu], stateb], writes=[stateb])
                        P.op("pool", lambda e: e.tensor_copy(out=sbf[:], in_=state[:]), reads=[stateb], writes=[sbfb])

                    def finish(blk):
                        csl = slice(blk * 128, (blk + 1) * 128)
                        P.op("pool", lambda e, blk=blk: e.tensor_tensor(out=y[:], in0=y[:], in1=sz[:, blk, :], op=ALU.mult),
                             reads=[yb, szb[blk]], writes=[yb])
                        P.op("act", lambda e: e.activation(out=yTc[:].rearrange("p c t -> p (c t)"), in_=y[:], func=AF.Square,
                                                           accum_out=ss[:]),
                             reads=[yb], writes=[yTcb, ssb])
                        P.op("act", lambda e: e.activation(out=lnss[:], in_=ss[:], func=AF.Ln, scale=1.0 / SSD_DI, bias=sm("eps")),
                             reads=[ssb, cbuf], writes=[lnssb])
                        P.op("act", lambda e: e.activation(out=rs[:], in_=lnss[:], func=AF.Exp, scale=-0.5),
                             reads=[lnssb], writes=[rsb])
                        P.op("dve", lambda e: e.tensor_scalar(out=y[:], in0=y[:], scalar1=rs[:, 0:1], scalar2=None, op0=ALU.mult),
                             reads=[yb, rsb], writes=[yb])
                        for c4 in range(4):
                            pt = ps_ring.next()
                            for q in range(4):
                                cc = 4 * c4 + q
                                P.op("pe", lambda e, pt=pt, q=q, cc=cc: e.transpose(
                                    psums[pt][:, q * 128:(q + 1) * 128], y[:, cc * 128:(cc + 1) * 128], ident32),
                                    reads=[yb, cbuf], writes=[psb[pt]])
                            for q in range(4):
                                cc = 4 * c4 + q
                                P.op("act" if q % 2 == 0 else "dve", (lambda e, pt=pt, q=q, cc=cc: e.activation(
                                    out=yTc[:, cc, :], in_=psums[pt][:, q * 128:(q + 1) * 128], func=AF.Identity,
                                    scale=sm("sng", j * 16 + cc))) if q % 2 == 0 else (lambda e, pt=pt, q=q, cc=cc: e.tensor_scalar(
                                    out=yTc[:, cc, :], in0=psums[pt][:, q * 128:(q + 1) * 128], scalar1=sm("sng", j * 16 + cc),
                                    scalar2=None, op0=ALU.mult)),
                                    reads=[psb[pt], cbuf], writes=[yTcb])
                        for k in range(4):
                            wt, wtb = next_w(("sout", j, k))
                            wv = v3(wt, 16, 256)
                            for nl in range(2):
                                pi = ps_ring2.next()
                                for cc in range(16):
                                    P.op("pe", lambda e, pi=pi, cc=cc, nl=nl, wv=wv: e.matmul(
                                        psums[pi][:, 0:128], wv[:, cc, nl * 128:(nl + 1) * 128], yTc[:, cc, :],
                                        start=(cc == 0), stop=(cc == 15)),
                                        reads=[wtb, yTcb], writes=[psb[pi]])
                                P.op("dve", lambda e, pi=pi, k=k, nl=nl, csl=csl, xt=xt: e.tensor_tensor(
                                    out=xt[:, 2 * k + nl, csl], in0=psums[pi][:, 0:128], in1=xt[:, 2 * k + nl, csl], op=ALU.add),
                                    reads=[psb[pi], xtb[xi]], writes=[xtb[xi]])

                    prep(0)
                    for blk in range(4):
                        groups(blk)
                        state_update(blk)
                        if blk + 1 < 4:
                            prep(blk + 1)
                        finish(blk)
                    store_x(dst_v, it, xi, dstbufs[it])
                P.barrier()

        def final_phase(src_v, srcbufs):
            ph = ExitStack()
            with ph:
                ots = [ph.enter_context(nc.sbuf_tensor(f"ot{i}", [128, DC, TT], F32)) for i in range(2)]
                otb = [Buf() for _ in range(2)]
                cur = load_x(src_v, 0, srcbufs[0])
                for it in range(NT):
                    xi = cur
                    if it + 1 < NT:
                        cur = load_x(src_v, it + 1, srcbufs[it + 1])
                    rms_stats(xi)
                    o = it % 2
                    rms_apply(xi, "fing", 0, ots[o], otb[o])
                    P.dma("pool", lambda e, o=o, it=it: e.dma_start(out=out_v[:, :, it * TT:(it + 1) * TT], in_=ots[o][:]),
                          f"os{o}", reads=[otb[o]], writes=[outb])
                P.barrier()

        outb = Buf("out")
        src_v, srcb = (xin_v, [None] * NT)
        for item in plan:
            if item[0] == "ffn":
                ffn_phase(item[1], src_v, srcb, xres_v, xres_b)
            if item[0] == "sb":
                sb_phase(item[1], item[2], src_v, srcb, xres_v, xres_b)
            if item[0] == "ssd":
                ssd_phase(item[1], item[2], src_v, srcb, xres_v, xres_b)
            src_v, srcb = xres_v, xres_b
        if do_final:
            final_phase(src_v, srcb)
        else:
            cur = None
            for it in range(NT):
                xi = load_x(src_v, it, srcb[it])
                P.dma("pool", lambda e, xi=xi, it=it: e.dma_start(out=out_v[:, :, it * TT:(it + 1) * TT], in_=xts[xi][:]),
                      f"os{xi}", reads=[xtb[xi]], writes=[outb])
            P.barrier()
        P.barrier()
        P.emit()
    return nc


def prep_inputs(inp):
    small, scols = pack_small(inp)
    eps = np.full((128, 1), NORM_EPS, np.float32)
    scols["eps"] = (small.shape[1], 1)
    scols["one"] = (small.shape[1] + 1, 1)
    small = np.concatenate([small, eps, np.ones((128, 1), np.float32)], axis=1)
    c16, c16cols, c32, c32cols = make_consts()
    c16cols = dict(c16cols)
    amask = c16cols.pop("_amask")
    return small, scols, c16, c16cols, c32, c32cols, amask


FULL_PLAN = [("ssd", 0, 0), ("ffn", 0), ("sb", 0, 1), ("ffn", 1), ("ssd", 1, 2), ("ffn", 2), ("sb", 1, 3), ("ffn", 3)]


def run_plan(inp, plan, S, x_batch, do_final=True, ncores=8, trace=False):
    small, scols, c16, c16cols, c32, c32cols, amask = prep_inputs(inp)
    nc = build_program(S, plan, scols, small.shape[1], c16cols, c16.shape[1], c32cols, c32.shape[1], do_final=do_final)
    kinds = set(p[0] for p in plan)
    shared = {"small": small, "c16": c16, "c32": c32, "amaskd": np.ascontiguousarray(amask)}
    if "ffn" in kinds:
        shared["ffn_w_in"] = np.ascontiguousarray(inp["ffn_w_in"], np.float32)
        shared["ffn_w_out"] = np.ascontiguousarray(inp["ffn_w_out"], np.float32)
    if "ssd" in kinds:
        shared["ssd_w_in"] = np.ascontiguousarray(inp["ssd_w_in"], np.float32)
        shared["ssd_w_out"] = np.ascontiguousarray(inp["ssd_w_out"], np.float32)
    if "sb" in kinds:
        shared["sb_w_qkv"] = np.ascontiguousarray(inp["sb_w_qkv"], np.float32)
        shared["sb_w_out"] = np.ascontiguousarray(inp["sb_w_out"], np.float32)
    in_maps = []
    for b in range(ncores):
        m = dict(shared)
        m["xT"] = np.ascontiguousarray(x_batch[b].T)
        in_maps.append(m)
    res = run_bass_kernel_spmd(nc, in_maps, core_ids=list(range(ncores)), trace=trace)
    out = np.stack([np.ascontiguousarray(r["outT"].T) for r in res.results], axis=0)
    return out, res


def kernel(**inputs):
    x = np.asarray(inputs["x"], np.float32)
    out, _ = run_plan(inputs, FULL_PLAN, x.shape[1], x)
    return out.astype(np.float32)
```
